# Optimizing a Trainium2 kernel written in Bass

```python
import jax, jax.numpy as jnp
from jax import lax
import numpy as np

D_MODEL = 1024
BATCH = 2
SEQ = 16384
DEPTH = 1
DEC_BATCH = 16
DEC_SEQ = 16
PAST_LEN = 4096

CHUNK = 64
D_MIX = 2 * D_MODEL
C_CONV = D_MIX // 2
CONV_W = 31
SSM_HEADS = 16
SSM_HEAD_DIM = (D_MIX // 2) // SSM_HEADS
D_SSM = SSM_HEADS * SSM_HEAD_DIM
SSM_GROUPS = 4
D_STATE = 128
SSM_CONV_W = 4
D_XBC = D_SSM + 2 * SSM_GROUPS * D_STATE
D_FF = ((8 * D_MODEL // 3 + 127) // 128) * 128
FFN_CONV_W = 3
D_IN_PROJ = 2 * C_CONV + D_SSM + D_XBC + SSM_HEADS
EPS = 1e-6

kernel_name = "hymba_style_conformer_conv_ssd_convffn_step"


def rmsnorm(x, g):
    x32 = x.astype(jnp.float32)
    y = x32 * lax.rsqrt(jnp.mean(x32 * x32, axis=-1, keepdims=True) + EPS)
    return (y * g.astype(jnp.float32)).astype(x.dtype)


def layernorm(x, g, b):
    x32 = x.astype(jnp.float32)
    mu = jnp.mean(x32, axis=-1, keepdims=True)
    xc = x32 - mu
    var = jnp.mean(xc * xc, axis=-1, keepdims=True)
    return (xc * lax.rsqrt(var + EPS) * g.astype(jnp.float32) + b.astype(jnp.float32)).astype(x.dtype)


def causal_dwconv(x, prefix, w, b):
    xp = jnp.concatenate([prefix.astype(x.dtype), x], axis=1)
    out = lax.conv_general_dilated(
        xp, w[:, None, :].astype(x.dtype), window_strides=(1,), padding="VALID",
        dimension_numbers=("NWC", "WIO", "NWC"), feature_group_count=x.shape[-1])
    return out + b.astype(x.dtype), xp[:, -(w.shape[0] - 1):]


def ssd_scan(x, dt, A, Bm, Cm, h0, block):
    b, l, h, p = x.shape
    g, n = Bm.shape[2], Bm.shape[3]
    r = h // g
    c = l // block
    x = x.reshape(b, c, block, g, r, p)
    dt = dt.reshape(b, c, block, g, r)
    a = dt * A.reshape(g, r)
    xdt = x * dt[..., None]
    Bm = Bm.reshape(b, c, block, g, n)
    Cm = Cm.reshape(b, c, block, g, n)
    a_cum = jnp.cumsum(a, axis=2)
    seg = a_cum[:, :, :, None] - a_cum[:, :, None, :]
    causal = jnp.tril(jnp.ones((block, block), dtype=bool))[None, None, :, :, None, None]
    decay = jnp.exp(jnp.where(causal, seg, -jnp.inf))
    cb = jnp.einsum('bctgn,bcsgn->bctsg', Cm, Bm)
    y_diag = jnp.einsum('bctsgr,bcsgrp->bctgrp', cb[..., None] * decay, xdt)
    decay_to_end = jnp.exp(a_cum[:, :, -1:] - a_cum)
    states = jnp.einsum('bcsgn,bcsgrp->bcgrpn', Bm, xdt * decay_to_end[..., None])
    block_decay = jnp.exp(a_cum[:, :, -1])

    def step(hc, inp):
        st, dec = inp
        return hc * dec[..., None, None] + st, hc

    h_final, h_prev = lax.scan(step, h0.reshape(b, g, r, p, n),
                               (jnp.moveaxis(states, 1, 0), jnp.moveaxis(block_decay, 1, 0)))
    h_prev = jnp.moveaxis(h_prev, 0, 1)
    y_off = jnp.einsum('bctgn,bcgrpn->bctgrp', Cm, h_prev) * jnp.exp(a_cum)[..., None]
    y = (y_diag + y_off).reshape(b, l, h, p)
    return y, h_final.reshape(b, h, p, n)


def hybrid_layer(x, buf_a, buf_xbc, h0, buf_ffn,
                 norm_mix, w_in, conv_a_w, conv_a_b, ln_a_g, ln_a_b,
                 ssm_conv_w, ssm_conv_b, dt_bias, a_log, d_skip, ssm_norm, w_out,
                 norm_ffn, w_up, ffn_conv_w, ffn_conv_b, w_down):
    bsz, l, _ = x.shape
    hn = rmsnorm(x, norm_mix)
    proj = hn @ w_in
    cuts = (C_CONV, 2 * C_CONV, 2 * C_CONV + D_SSM, 2 * C_CONV + D_SSM + D_XBC)
    a_val, a_gate, z, xbc, dt_raw = jnp.split(proj, cuts, axis=-1)
    u = a_val * jax.nn.sigmoid(a_gate)
    u_conv, new_buf_a = causal_dwconv(u, buf_a, conv_a_w, conv_a_b)
    out_a = jax.nn.silu(layernorm(u_conv, ln_a_g, ln_a_b))
    xbc_c, new_buf_xbc = causal_dwconv(xbc, buf_xbc, ssm_conv_w, ssm_conv_b)
    xbc_c = jax.nn.silu(xbc_c).astype(jnp.float32)
    xs, Bm, Cm = jnp.split(xbc_c, (D_SSM, D_SSM + SSM_GROUPS * D_STATE), axis=-1)
    xs = xs.reshape(bsz, l, SSM_HEADS, SSM_HEAD_DIM)
    Bm = Bm.reshape(bsz, l, SSM_GROUPS, D_STATE)
    Cm = Cm.reshape(bsz, l, SSM_GROUPS, D_STATE)
    dt = jax.nn.softplus(dt_raw.astype(jnp.float32) + dt_bias.astype(jnp.float32))
    A = -jnp.exp(a_log.astype(jnp.float32))
    block = CHUNK if l % CHUNK == 0 else l
    y, new_h = ssd_scan(xs, dt, A, Bm, Cm, h0.astype(jnp.float32), block)
    y = y + d_skip.astype(jnp.float32)[:, None] * xs
    y = y.reshape(bsz, l, D_SSM).astype(x.dtype)
    out_b = rmsnorm(y * jax.nn.silu(z), ssm_norm)
    x = x + jnp.concatenate([out_a, out_b], axis=-1) @ w_out
    hf = rmsnorm(x, norm_ffn)
    gate, val = jnp.split(hf @ w_up, 2, axis=-1)
    gate_c, new_buf_ffn = causal_dwconv(gate, buf_ffn, ffn_conv_w, ffn_conv_b)
    x = x + (jax.nn.silu(gate_c) * val) @ w_down
    return x, new_buf_a, new_buf_xbc, new_h, new_buf_ffn


def setup_inputs(seed: int = 0) -> dict:
    key = jax.random.key(seed)
    k = jax.random.split(key, 32)
    f32 = jnp.float32

    def nrm(kk, shape, scale):
        return jax.random.normal(kk, shape, f32) * scale

    def gain(kk, shape):
        return 1.0 + 0.02 * jax.random.normal(kk, shape, f32)

    dt0 = jnp.exp(jax.random.uniform(k[0], (DEPTH, SSM_HEADS), f32, np.log(1e-3), np.log(1e-1)))
    dt_bias = dt0 + jnp.log(-jnp.expm1(-dt0))
    a_log = jnp.log(jax.random.uniform(k[1], (DEPTH, SSM_HEADS), f32, 1.0, 16.0))
    return {
        "x_prompt": nrm(k[2], (BATCH, SEQ, D_MODEL), 1.0),
        "x_sample": nrm(k[3], (DEC_BATCH, DEC_SEQ, D_MODEL), 1.0),
        "cache_conv_a": nrm(k[4], (DEPTH, DEC_BATCH, CONV_W - 1, C_CONV), 0.5),
        "cache_ssm_conv": nrm(k[5], (DEPTH, DEC_BATCH, SSM_CONV_W - 1, D_XBC), 1.0),
        "state_ssm": nrm(k[6], (DEPTH, DEC_BATCH, SSM_HEADS, SSM_HEAD_DIM, D_STATE), 0.1),
        "cache_ffn_conv": nrm(k[7], (DEPTH, DEC_BATCH, FFN_CONV_W - 1, D_FF), 1.0),
        "norm_mix": gain(k[8], (DEPTH, D_MODEL)),
        "w_in": nrm(k[9], (DEPTH, D_MODEL, D_IN_PROJ), D_MODEL ** -0.5),
        "conv_a_w": nrm(k[10], (DEPTH, CONV_W, C_CONV), CONV_W ** -0.5),
        "conv_a_b": nrm(k[11], (DEPTH, C_CONV), 0.02),
        "ln_a_g": gain(k[12], (DEPTH, C_CONV)),
        "ln_a_b": nrm(k[13], (DEPTH, C_CONV), 0.02),
        "ssm_conv_w": nrm(k[14], (DEPTH, SSM_CONV_W, D_XBC), SSM_CONV_W ** -0.5),
        "ssm_conv_b": nrm(k[15], (DEPTH, D_XBC), 0.02),
        "dt_bias": dt_bias,
        "a_log": a_log,
        "d_skip": gain(k[16], (DEPTH, SSM_HEADS)),
        "ssm_norm": gain(k[17], (DEPTH, D_SSM)),
        "w_out": nrm(k[18], (DEPTH, D_MIX, D_MODEL), D_MIX ** -0.5),
        "norm_ffn": gain(k[19], (DEPTH, D_MODEL)),
        "w_up": nrm(k[20], (DEPTH, D_MODEL, 2 * D_FF), D_MODEL ** -0.5),
        "ffn_conv_w": nrm(k[21], (DEPTH, FFN_CONV_W, D_FF), FFN_CONV_W ** -0.5),
        "ffn_conv_b": nrm(k[22], (DEPTH, D_FF), 0.02),
        "w_down": nrm(k[23], (DEPTH, D_FF, D_MODEL), D_FF ** -0.5),
        "norm_final": gain(k[24], (D_MODEL,)),
    }


def reference(x_prompt, x_sample, cache_conv_a, cache_ssm_conv, state_ssm, cache_ffn_conv,
              norm_mix, w_in, conv_a_w, conv_a_b, ln_a_g, ln_a_b,
              ssm_conv_w, ssm_conv_b, dt_bias, a_log, d_skip, ssm_norm, w_out,
              norm_ffn, w_up, ffn_conv_w, ffn_conv_b, w_down, norm_final):
    bp = x_prompt.shape[0]
    xp, xs = x_prompt, x_sample
    pa, pb, ph, pf = [], [], [], []
    sa, sb, sh, sf = [], [], [], []
    for i in range(DEPTH):
        wts = (norm_mix[i], w_in[i], conv_a_w[i], conv_a_b[i], ln_a_g[i], ln_a_b[i],
               ssm_conv_w[i], ssm_conv_b[i], dt_bias[i], a_log[i], d_skip[i], ssm_norm[i], w_out[i],
               norm_ffn[i], w_up[i], ffn_conv_w[i], ffn_conv_b[i], w_down[i])
        xp, na, nb, nh, nf = hybrid_layer(
            xp,
            jnp.zeros((bp, CONV_W - 1, C_CONV), xp.dtype),
            jnp.zeros((bp, SSM_CONV_W - 1, D_XBC), xp.dtype),
            jnp.zeros((bp, SSM_HEADS, SSM_HEAD_DIM, D_STATE), jnp.float32),
            jnp.zeros((bp, FFN_CONV_W - 1, D_FF), xp.dtype),
            *wts)
        pa.append(na); pb.append(nb); ph.append(nh); pf.append(nf)
        xs, ma, mb, mh, mf = hybrid_layer(
            xs, cache_conv_a[i], cache_ssm_conv[i], state_ssm[i], cache_ffn_conv[i], *wts)
        sa.append(ma); sb.append(mb); sh.append(mh); sf.append(mf)
    y_prompt = rmsnorm(xp, norm_final)
    y_sample = rmsnorm(xs, norm_final)
    return (y_prompt, y_sample,
            jnp.stack(pa), jnp.stack(pb), jnp.stack(ph), jnp.stack(pf),
            jnp.stack(sa), jnp.stack(sb), jnp.stack(sh), jnp.stack(sf))
```

```python
import contextlib
import numpy as np
import concourse.bass as bass
import concourse.mybir as mybir
from concourse.bass_utils import run_bass_kernel_spmd

F32 = mybir.dt.float32
BF16 = mybir.dt.bfloat16
AF = mybir.ActivationFunctionType
ALU = mybir.AluOpType

ENGS = ("pe", "act", "dve", "pool", "sp")
N_DMA_SEMS = 8

D = 1024
DIN = 5136
DFF = 2816
KC = 8
CA = 8
CX = 16
CF = 22
NH = 16
HP = 64
NG = 4
HA, HX, HG = 30, 3, 2
WA, WX, WF = 31, 4, 3
EPS = 1e-6
SEQ = 16384
NSEG = 4
SEGTOK = SEQ // NSEG


class Reg:
    __slots__ = ("space", "a", "b")

    def __init__(self, space, a, b):
        self.space, self.a, self.b = space, a, b


class Op:
    __slots__ = ("eng", "fn", "is_dma", "deps", "idx", "ms", "dsem", "dcount", "needed", "name", "cc")


class Sched:
    def __init__(self, nc, same_engine_sync=True):
        self.nc = nc
        self.ops = []
        self.spaces = {}
        self.same_engine_sync = same_engine_sync
        self.nosync_engines = ()

    def _touch(self, op, reg, write):
        ivs = self.spaces.setdefault(reg.space, [])
        a, b = reg.a, reg.b
        if reg.space == "ps":
            a = a // 2048 * 2048
            b = (b + 2047) // 2048 * 2048
            write = True
        new = []
        for iv in ivs:
            if iv[1] <= a or iv[0] >= b:
                new.append(iv)
                continue
            if iv[2] is not None:
                op.deps.add(iv[2])
            if write:
                for r in iv[3]:
                    op.deps.add(r)
                if iv[0] < a:
                    new.append([iv[0], a, iv[2], list(iv[3])])
                if iv[1] > b:
                    new.append([b, iv[1], iv[2], list(iv[3])])
            else:
                iv[3].append(op)
                new.append(iv)
        if write:
            new.append([a, b, op, []])
        else:
            covered = sorted((max(iv[0], a), min(iv[1], b)) for iv in new
                             if not (iv[1] <= a or iv[0] >= b))
            cur = a
            for (s, e) in covered:
                if s > cur:
                    new.append([cur, s, None, [op]])
                cur = max(cur, e)
            if cur < b:
                new.append([cur, b, None, [op]])
        self.spaces[reg.space] = new

    def add(self, eng, fn, reads=(), writes=(), dma=False, name="", cc=False):
        op = Op()
        op.eng, op.fn, op.is_dma, op.name = eng, fn, (dma or cc), name
        op.cc = cc
        op.deps = set()
        op.idx = len(self.ops)
        op.ms = None
        op.needed = False
        for r in reads:
            self._touch(op, r, False)
        for w in writes:
            self._touch(op, w, True)
        if cc:
            self._touch(op, Reg("dma_barrier", 0, 1), True)
        elif dma:
            self._touch(op, Reg("dma_barrier", 0, 1), False)
        op.deps.discard(op)
        self.ops.append(op)
        return op

    def emit(self):
        nc = self.nc
        streams = {e: [] for e in ENGS}
        for op in self.ops:
            streams[op.eng].append(op)
        dma_cnt = {e: 0 for e in ENGS}
        dma_prev = {}
        n_cc = 0
        for op in self.ops:
            if op.is_dma and op.cc:
                prev = dma_prev.get(("cc", 0))
                op.dsem = ("cc", 0)
                op.dcount = (prev.dcount if prev is not None else 0) + 1
                if prev is not None:
                    op.deps.add(prev)
                dma_prev[("cc", 0)] = op
                n_cc += 1
            elif op.is_dma:
                slot = dma_cnt[op.eng] % N_DMA_SEMS
                dma_cnt[op.eng] += 1
                prev = dma_prev.get((op.eng, slot))
                op.dsem = (op.eng, slot)
                op.dcount = (prev.dcount if prev is not None else 0) + 16
                if prev is not None:
                    op.deps.add(prev)
                dma_prev[(op.eng, slot)] = op

        def skip(d, op):
            if d.is_dma:
                return False
            if op.is_dma:
                return False
            if d.eng == "pe" and op.eng == "pe":
                return True
            if d.eng == op.eng and d.eng in self.nosync_engines:
                return True
            return False

        for op in self.ops:
            for d in op.deps:
                if d.is_dma or skip(d, op):
                    continue
                d.needed = True
        ms_cnt = {e: 0 for e in ENGS}
        for op in self.ops:
            if not op.is_dma and op.needed:
                ms_cnt[op.eng] += 1
                op.ms = ms_cnt[op.eng]
        self.stats = dict(n_ops=len(self.ops), ms=dict(ms_cnt), dma=dict(dma_cnt))

        with contextlib.ExitStack() as st:
            esem = {e: st.enter_context(nc.semaphore("s_" + e)) for e in ENGS}
            dsem = {}
            for e in ENGS:
                for k in range(N_DMA_SEMS):
                    if dma_cnt[e] > k:
                        dsem[(e, k)] = st.enter_context(nc.semaphore("d_%s%d" % (e, k)))
            if n_cc:
                dsem[("cc", 0)] = st.enter_context(nc.semaphore("d_cc"))
            final_waits = {}
            for op in self.ops:
                if op.is_dma:
                    final_waits[op.dsem] = max(final_waits.get(op.dsem, 0), op.dcount)

            waited = {e: {} for e in ENGS}
            know = {}
            plan = {}

            def keyval(d):
                if d.is_dma:
                    return ("d",) + d.dsem, d.dcount
                return ("e", d.eng), d.ms
            for op in self.ops:
                w = waited[op.eng]
                needs, prod = {}, {}
                for d in op.deps:
                    if not d.is_dma and skip(d, op):
                        continue
                    k, v = keyval(d)
                    if needs.get(k, 0) < v:
                        needs[k] = v
                        prod[k] = d
                lst = []
                for k in sorted(needs, key=lambda kk: -prod[kk].idx):
                    v = needs[k]
                    if w.get(k, 0) >= v:
                        continue
                    lst.append((k, v))
                    w[k] = v
                    kd = know.get(prod[k])
                    if kd:
                        for kk, vv in kd.items():
                            if w.get(kk, 0) < vv:
                                w[kk] = vv
                plan[op] = lst
                if op.is_dma or op.ms is not None:
                    snap = dict(w)
                    k, v = keyval(op)
                    snap[k] = v
                    know[op] = snap
            self.stats["waits"] = sum(len(v) for v in plan.values())

            def run_stream(ename, h):
                for op in streams[ename]:
                    for (k, val) in plan[op]:
                        s_ = esem[k[1]] if k[0] == "e" else dsem[(k[1], k[2])]
                        h.wait_ge(s_, val)
                    inst = op.fn(h)
                    if op.is_dma:
                        inst.then_inc(dsem[op.dsem], 1 if op.cc else 16)
                    elif op.ms is not None:
                        inst.then_inc(esem[ename], 1)
                if ename == "sp":
                    for key, val in final_waits.items():
                        h.wait_ge(dsem[key], val)

            with nc.Block() as block:
                @block.sync
                def _(h):
                    run_stream("sp", h)

                @block.tensor
                def _(h):
                    run_stream("pe", h)

                @block.scalar
                def _(h):
                    run_stream("act", h)

                @block.vector
                def _(h):
                    run_stream("dve", h)

                @block.gpsimd
                def _(h):
                    run_stream("pool", h)


class Arena:
    def __init__(self, space, tensor_f32, nbytes):
        self.space, self.t, self.nbytes, self.off = space, tensor_f32, nbytes, 0

    def alloc(self, nelem, dtype, align=32):
        esz = 4 if dtype == F32 else 2
        self.off = (self.off + align - 1) // align * align
        a = self.off
        self.off += nelem * esz
        assert self.off <= self.nbytes, "arena overflow %s %d > %d" % (self.space, self.off, self.nbytes)
        return Buf(self, a, nelem, dtype)

    def at(self, byte_off, nelem, dtype):
        return Buf(self, byte_off, nelem, dtype)


class Buf:
    def __init__(self, arena, byte_off, nelem, dtype):
        self.arena, self.off, self.n, self.dtype = arena, byte_off, nelem, dtype
        self.esz = 4 if dtype == F32 else 2
        assert byte_off % 4 == 0
        w0 = byte_off // 4
        w1 = (byte_off + nelem * self.esz + 3) // 4
        ap = arena.t[:, w0:w1]
        if dtype != F32:
            ap = ap.bitcast(dtype)[:, 0:nelem]
        self.ap = ap

    def reg(self, e0=0, e1=None):
        if e1 is None:
            e1 = self.n
        return Reg(self.arena.space, self.off + e0 * self.esz, self.off + e1 * self.esz)

    def v3(self, b):
        return self.ap.rearrange("p (a b) -> p a b", b=b)

    def v4(self, b, c):
        return self.ap.rearrange("p (a b c) -> p a b c", b=b, c=c)


def bc(ap, axis, shape):
    return ap.unsqueeze(axis).to_broadcast(shape)


def build_nc(cfg):
    kinds = cfg["blocks"]
    groups = cfg["groups"]
    NBLK = len(kinds)
    NPRE = cfg.get("npre", 0)
    NBMAX = max(len(g) for g in groups)
    NMAX = NBMAX * 128
    last_main = max(i for i, k in enumerate(kinds) if k == "M")

    nc = bass.Bass("TRN2", target_bir_lowering=False)

    def din(name, shape, dt=F32):
        return nc.dram_tensor(name, list(shape), dt, kind="ExternalInput").ap()

    def dout(name, shape, dt=F32):
        return nc.dram_tensor(name, list(shape), dt, kind="ExternalOutput").ap()

    xs = din("xs", [NBLK * 128, D])
    tmask_d = din("tmask", [128, NBLK])
    gkeep_d = din("gkeep", [128, 1])
    c_conv_a = din("c_conv_a", [2, HA, D])
    c_ssm = din("c_ssm", [2, HX, 2048])
    c_state = din("c_state", [2, NH * HP, 128])
    c_ffn = din("c_ffn", [2, HG, DFF])
    w_in_d = din("w_in", [D, DIN])
    w_out_d = din("w_out", [2048, D])
    w_up_d = din("w_up", [D, 2 * DFF])
    w_down_d = din("w_down", [DFF, D])
    norm_mix_d = din("norm_mix", [128, KC])
    norm_ffn_d = din("norm_ffn", [128, KC])
    ssm_norm_d = din("ssm_norm", [128, KC])
    norm_final_d = din("norm_final", [1, D])
    wA_d = din("wA", [128, CA * WA])
    bA_d = din("bA", [128, CA])
    lng_d = din("lng", [128, CA])
    lnb_d = din("lnb", [128, CA])
    wS_d = din("wS", [128, CX * WX])
    bS_d = din("bS", [128, CX])
    wF_d = din("wF", [128, CF * WF])
    bF_d = din("bF", [128, CF])
    hvec_d = din("hvec", [1, 3 * NH])
    ident_d = din("ident", [128, 128])
    triu_d = din("triu", [128, 128])
    lmat_d = din("lmat", [128, 128])
    NCORES_ALL = cfg.get("ncores", 8)
    NCORES = cfg.get("nseg", 4)
    RGROUPS = [list(range(g * NCORES, (g + 1) * NCORES)) for g in range(NCORES_ALL // NCORES)]
    if NPRE:
        pmask_d = din("pmask", [128, NPRE])
        cmask_d = din("cmask", [128, NCORES])
        xpre = xs[(NBLK - NPRE) * 128:, :]
        cc_in = nc.dram_tensor("cc_in", [129, D], F32, kind="Internal").ap()
        cc_out = nc.dram_tensor("cc_out", [NCORES * 129, D], F32, kind="Internal").ap()

    y_d = dout("y", [NBLK * 128, D])
    o_conv_a_s = dout("o_conv_a_s", [2, HA, D])
    o_ssm_s = dout("o_ssm_s", [2, HX, 2048])
    o_state_s = dout("o_state_s", [2, NH * HP, 128])
    o_ffn_s = dout("o_ffn_s", [2, HG, DFF])
    o_conv_a_p = dout("o_conv_a_p", [HA, D])
    o_ssm_p = dout("o_ssm_p", [HX, 2048])
    o_state_p = dout("o_state_p", [NH * HP, 128])
    o_ffn_p = dout("o_ffn_p", [HG, DFF])

    wb_in = nc.dram_tensor("wb_in", [D, DIN], BF16, kind="Internal").ap()
    wb_out = nc.dram_tensor("wb_out", [2048, D], BF16, kind="Internal").ap()
    wb_up = nc.dram_tensor("wb_up", [D, 2 * DFF], BF16, kind="Internal").ap()
    wb_down = nc.dram_tensor("wb_down", [DFF, D], BF16, kind="Internal").ap()
    hpre_d = nc.dram_tensor("hpre_d", [128, D], F32, kind="Internal").ap()
    dg_dram = {"A": nc.dram_tensor("dgA", [CA, 128, WA * 128], BF16, kind="Internal").ap(),
               "S": nc.dram_tensor("dgS", [CX, 128, WX * 128], BF16, kind="Internal").ap(),
               "F": nc.dram_tensor("dgF", [CF, 128, WF * 128], BF16, kind="Internal").ap()}

    SB_BYTES = 207 * 1024
    with contextlib.ExitStack() as st:
        sbt = st.enter_context(nc.sbuf_tensor("arena", [128, SB_BYTES // 4], F32))
        pst = st.enter_context(nc.psum_tensor("psum", [128, 4096], F32))
        sb = Arena("sb", sbt, SB_BYTES)
        psA = Arena("ps", pst, 16384)
        S = Sched(nc)
        S.nosync_engines = cfg.get("nosync", ())

        ps_state = {"bank": 0}

        def PS(nelem, dtype=F32, nbanks=1):
            b = ps_state["bank"]
            if b + nbanks > 8:
                b = 0
            ps_state["bank"] = (b + nbanks) % 8
            return psA.at(b * 2048, nelem, dtype)

        identf = sb.alloc(128, F32)
        identb = sb.alloc(128, BF16)
        triu = sb.alloc(128, F32)
        lmat = sb.alloc(128, F32)
        onesf = sb.alloc(128, F32)
        onesb = sb.alloc(128, BF16)
        nfin = sb.alloc(D, F32)
        g_mix = sb.alloc(KC, F32)
        g_ffn = sb.alloc(KC, F32)
        g_ssm = sb.alloc(KC, F32)
        wA = sb.alloc(CA * WA, F32)
        bA = sb.alloc(CA, F32)
        lng = sb.alloc(CA, F32)
        lnb = sb.alloc(CA, F32)
        wS = sb.alloc(CX * WX, F32)
        bS = sb.alloc(CX, F32)
        wF = sb.alloc(CF * WF, F32)
        bF = sb.alloc(CF, F32)
        hvec = sb.alloc(3 * NH, F32)
        tmask = sb.alloc(NBLK, F32)
        gkeep = sb.alloc(1, F32)
        epsc = sb.alloc(1, F32)
        onec = sb.alloc(1, F32)
        if NPRE:
            pmask = sb.alloc(NPRE, F32)
            cmask = sb.alloc(NCORES, F32)
            atacc = sb.alloc(NH, F32)
        uh = sb.alloc(CA * HA, BF16)
        xh = sb.alloc(CX * HX, BF16)
        gh = sb.alloc(CF * HG, BF16)
        hT = sb.alloc(D, F32)
        hTb = sb.alloc(D, BF16)
        pxh = sb.alloc(12 * HX, BF16)
        NSLOT = 3
        wslots = [sb.alloc(KC * 512, BF16) for _ in range(NSLOT)]
        xg = sb.alloc(NBMAX * D, F32)
        diagA = [sb.alloc(WA * 128, BF16) for _ in range(2)]
        diagS = [sb.alloc(WX * 128, BF16) for _ in range(2)]
        diagF = [sb.alloc(WF * 128, BF16) for _ in range(2)]
        small = sb.alloc(512, F32)
        R1 = sb.alloc(KC * NMAX, BF16)
        hnT = R1
        outa = R1
        R2 = sb.alloc(CA * (NMAX + HA), BF16)
        u_ext = R2
        outb = None
        base_ffn = (sb.off + 31) // 32 * 32
        xbc_ext = sb.alloc(CX * (NMAX + HX), BF16)
        sz = sb.alloc(NBMAX * D, F32)
        CT = sb.alloc(NG * NMAX, BF16)
        BT = sb.alloc(NG * NMAX, BF16)
        Btok = sb.alloc(NBMAX * NG * 128, BF16)
        base_ln = (sb.off + 31) // 32 * 32
        xtok = sb.alloc(NBMAX * D, F32)
        rhs_seg = sb.alloc(NH * 128, F32)
        Emat = sb.alloc(NH * 128, F32)
        MT = sb.alloc(NH * 128, BF16)
        cbTm = sb.alloc(NG * 128, F32)
        xdt = sb.alloc(D, BF16)
        xdts = sb.alloc(D, BF16)
        t1 = sb.alloc(D, F32)
        t2 = sb.alloc(D, F32)
        t1b = sb.alloc(D, F32)
        ygn = sb.alloc(D, BF16)
        end_ssd = sb.off
        xnb = sb.at(cbTm.off, D, BF16)
        tmpf = [sb.alloc(NMAX, F32) for _ in range(2)]
        cacheTok = sb.at(t1.off, D, F32)
        tailb = sb.at(t2.off, D, F32)
        outb = sb.alloc(KC * NMAX, BF16)
        gate_ext = sb.at(base_ffn, CF * (NMAX + HG), BF16)
        actT = sb.at((gate_ext.off + gate_ext.n * 2 + 31) // 32 * 32, CF * NMAX, BF16)
        assert actT.off + actT.n * 2 <= end_ssd, "ffn region overflow"
        ucf = sb.at(xbc_ext.off, CA * NMAX, F32)
        assert ucf.n * 4 <= xbc_ext.n * 2
        ucb = sb.at(rhs_seg.off, CA * NMAX, BF16)
        sqb = sb.at(ucb.off + ucb.n * 2, CA * NMAX, BF16)
        assert sqb.off + sqb.n * 2 <= MT.off, "ln region overflow"
        yst = sb.at(sz.off, NBMAX * D, F32)
        mu = tmpf[0]
        rstdN = tmpf[1]

        def DMA(q, out_ap, in_ap, reads=(), writes=(), name=""):
            S.add(q, lambda h, o=out_ap, i=in_ap: h.dma_start(out=o, in_=i), reads=reads, writes=writes,
                  dma=True, name=name)

        def MM(out_ap, pairs, reads, writes, start=True, stop=True, name=""):
            def fn(h, out_ap=out_ap, pairs=pairs, start=start, stop=stop):
                n = len(pairs)
                inst = None
                for i, (l, r) in enumerate(pairs):
                    inst = h.matmul(out_ap, lhsT=l, rhs=r, start=(start and i == 0), stop=(stop and i == n - 1))
                return inst
            S.add("pe", fn, reads=reads, writes=writes, name=name)

        def MMS(items, reads, writes, name=""):
            def fn(h, items=items):
                inst = None
                for (o, l, r) in items:
                    inst = h.matmul(o, lhsT=l, rhs=r, start=True, stop=True)
                return inst
            S.add("pe", fn, reads=reads, writes=writes, name=name)

        def TR(items, reads, writes, name=""):
            def fn(h, items=items):
                inst = None
                for (o, i, idn) in items:
                    inst = h.transpose(out=o, in_=i, identity=idn)
                return inst
            S.add("pe", fn, reads=reads, writes=writes, name=name)

        def ACT(out_ap, in_ap, func, reads, writes, bias=None, scale=None, accum=None, name=""):
            kw = {}
            if bias is not None:
                kw["bias"] = bias
            if scale is not None:
                kw["scale"] = scale
            if accum is not None:
                kw["accum_out"] = accum
            S.add("act", lambda h, o=out_ap, i=in_ap, f=func, kw=kw: h.activation(out=o, in_=i, func=f, **kw),
                  reads=reads, writes=writes, name=name)

        def TT(eng, out_ap, in0, in1, op, reads, writes, name=""):
            S.add(eng, lambda h, o=out_ap, a=in0, b=in1, op=op: h.tensor_tensor(out=o, in0=a, in1=b, op=op),
                  reads=reads, writes=writes, name=name)

        def TS(eng, out_ap, in0, s1, op0, reads, writes, s2=None, op1=None, name=""):
            def fn(h, o=out_ap, a=in0, s1=s1, s2=s2, op0=op0, op1=op1):
                if op1 is None:
                    return h.tensor_scalar(out=o, in0=a, scalar1=s1, scalar2=None, op0=op0)
                return h.tensor_scalar(out=o, in0=a, scalar1=s1, scalar2=s2, op0=op0, op1=op1)
            S.add(eng, fn, reads=reads, writes=writes, name=name)

        def STT(eng, out_ap, in0, scalar, in1, op0, op1, reads, writes, name=""):
            S.add(eng, lambda h, o=out_ap, a=in0, s=scalar, b=in1, op0=op0, op1=op1:
                  h.scalar_tensor_tensor(out=o, in0=a, scalar=s, in1=b, op0=op0, op1=op1),
                  reads=reads, writes=writes, name=name)

        def CP(eng, out_ap, in_ap, reads, writes, name=""):
            if eng == "act":
                S.add("act", lambda h, o=out_ap, i=in_ap: h.copy(out=o, in_=i), reads=reads, writes=writes, name=name)
            else:
                S.add(eng, lambda h, o=out_ap, i=in_ap: h.tensor_copy(out=o, in_=i), reads=reads, writes=writes,
                      name=name)

        def MEMSET(eng, buf, val):
            S.add(eng, lambda h, b=buf, v=val: h.memset(b.ap, v), writes=[buf.reg()])

        def dreg(name, a=0, b=1 << 30):
            return Reg("dram:" + name, a, b)

        for (buf, src) in ((identf, ident_d), (triu, triu_d), (lmat, lmat_d), (g_mix, norm_mix_d),
                           (g_ffn, norm_ffn_d), (g_ssm, ssm_norm_d), (wA, wA_d), (bA, bA_d), (lng, lng_d),
                           (lnb, lnb_d), (wS, wS_d), (bS, bS_d), (wF, wF_d), (bF, bF_d), (tmask, tmask_d),
                           (gkeep, gkeep_d)):
            DMA("sp", buf.ap, src, writes=[buf.reg()])
        if NPRE:
            DMA("sp", pmask.ap, pmask_d, writes=[pmask.reg()])
            DMA("sp", cmask.ap, cmask_d, writes=[cmask.reg()])
            MEMSET("dve", atacc, 0.0)
        DMA("sp", nfin.ap, norm_final_d.partition_broadcast(128), writes=[nfin.reg()])
        DMA("sp", hvec.ap, hvec_d.partition_broadcast(128), writes=[hvec.reg()])
        MEMSET("dve", onesf, 1.0)
        MEMSET("dve", onesb, 1.0)
        MEMSET("dve", epsc, EPS)
        MEMSET("dve", onec, 1.0)
        MEMSET("dve", uh, 0.0)
        MEMSET("dve", xh, 0.0)
        MEMSET("dve", gh, 0.0)
        MEMSET("dve", hT, 0.0)
        MEMSET("dve", hTb, 0.0)
        CP("dve", identb.ap, identf.ap, [identf.reg()], [identb.reg()])
        ACT(hvec.ap[:, NH:2 * NH], hvec.ap[:, NH:2 * NH], AF.Exp, [hvec.reg()], [hvec.reg()])
        TS("dve", hvec.ap[:, NH:2 * NH], hvec.ap[:, NH:2 * NH], -1.0, ALU.mult, [hvec.reg()], [hvec.reg()])
        dtb_ap = hvec.ap[:, 0:NH]
        A_ap = hvec.ap[:, NH:2 * NH]
        Dsk_ap = hvec.ap[:, 2 * NH:3 * NH]

        def convert(src, dst, name, rows, rb=128):
            for r0 in range(0, rows, rb):
                r1 = min(rows, r0 + rb)
                DMA("pool", dst[r0:r1, :], src[r0:r1, :], writes=[dreg(name, r0, r1)])
        for r0 in range(0, D, 256):
            DMA("pool", wb_in[r0:r0 + 256, 3072:DIN], w_in_d[r0:r0 + 256, 3072:DIN], writes=[dreg("wb_inx", r0, r0 + 256)])
        conv_q = []
        for r0 in range(0, D, 128):
            conv_q.append((wb_in[r0:r0 + 128, 0:3072], w_in_d[r0:r0 + 128, 0:3072], dreg("wb_in", r0, r0 + 128)))
        for (src_, dst_, nm_, rows_) in ((w_out_d, wb_out, "wb_out", 2048), (w_up_d, wb_up, "wb_up", D),
                                         (w_down_d, wb_down, "wb_down", DFF)):
            for r0 in range(0, rows_, 128):
                r1 = min(rows_, r0 + 128)
                conv_q.append((dst_[r0:r1, :], src_[r0:r1, :], dreg(nm_, r0, r1)))

        def emit_conversions(n):
            for _ in range(n):
                if conv_q:
                    o_, i_, w_ = conv_q.pop(0)
                    DMA("pool", o_, i_, writes=[w_])

        wstate = {"i": 0}

        def load_piece(wb, name, r0, nk, c0, ncols):
            slot = wslots[wstate["i"] % NSLOT]
            wstate["i"] += 1
            src = wb[r0:r0 + nk * 128, c0:c0 + ncols].rearrange("(k p) n -> p k n", p=128)
            dstv = slot.ap[:, 0:nk * ncols].rearrange("p (k n) -> p k n", n=ncols)
            rd_ = [dreg(name, r0, r0 + nk * 128)]
            if name == "wb_in":
                rd_.append(dreg("wb_inx", r0, r0 + nk * 128))
            DMA("sp", dstv, src, reads=rd_, writes=[slot.reg(0, nk * ncols)])
            return dstv, slot.reg(0, nk * ncols)

        xnbs = [xnb, ygn]

        def norm_transpose_group(nb, gcol, dstT, ncolsT):
            for b in range(nb):
                ACT(MT.ap[:, 0:D], xg.ap[:, b * D:(b + 1) * D], AF.Square, [xg.reg(b * D, (b + 1) * D)],
                    [MT.reg(0, D), small.reg(b, b + 1)], scale=1.0 / 32.0, accum=small.ap[:, b:b + 1])
            ACT(small.ap[:, 4:4 + nb], small.ap[:, 0:nb], AF.Ln, [small.reg(0, nb), epsc.reg()], [small.reg(4, 4 + nb)],
                bias=epsc.ap)
            ACT(small.ap[:, 8:8 + nb], small.ap[:, 4:4 + nb], AF.Exp, [small.reg(4, 4 + nb)], [small.reg(8, 8 + nb)],
                scale=-0.5)
            for b in range(nb):
                xb = xnbs[b % 2]
                TS("dve", xb.ap, xg.ap[:, b * D:(b + 1) * D], small.ap[:, 8 + b:9 + b], ALU.mult,
                   [xg.reg(b * D, (b + 1) * D), small.reg(8 + b, 9 + b)], [xb.reg()])
                pT = PS(KC * 128, BF16)
                TR([(pT.ap[:, k * 128:(k + 1) * 128], xb.ap[:, k * 128:(k + 1) * 128], identb.ap) for k in range(KC)],
                   [xb.reg(), identb.reg()], [pT.reg()])
                dv = dstT.ap.rearrange("p (k n) -> p k n", n=ncolsT)[:, :, b * 128:(b + 1) * 128]
                wregs = [dstT.reg(k * ncolsT + b * 128, k * ncolsT + (b + 1) * 128) for k in range(KC)]
                TT("dve", dv, pT.v3(128), bc(gcol.ap, 2, [128, KC, 128]), ALU.mult, [pT.reg(), gcol.reg()], wregs)

        def norm_transpose_block(b, gcol, dstT, ncolsT):
            o = 24 + 3 * b
            xb = xnbs[b % 2]
            ACT(MT.ap[:, 0:D], xg.ap[:, b * D:(b + 1) * D], AF.Square, [xg.reg(b * D, (b + 1) * D)],
                [MT.reg(0, D), small.reg(o, o + 1)], scale=1.0 / 32.0, accum=small.ap[:, o:o + 1])
            ACT(small.ap[:, o + 1:o + 2], small.ap[:, o:o + 1], AF.Ln, [small.reg(o, o + 1), epsc.reg()],
                [small.reg(o + 1, o + 2)], bias=epsc.ap)
            ACT(small.ap[:, o + 2:o + 3], small.ap[:, o + 1:o + 2], AF.Exp, [small.reg(o + 1, o + 2)],
                [small.reg(o + 2, o + 3)], scale=-0.5)
            TS("dve", xb.ap, xg.ap[:, b * D:(b + 1) * D], small.ap[:, o + 2:o + 3], ALU.mult,
               [xg.reg(b * D, (b + 1) * D), small.reg(o + 2, o + 3)], [xb.reg()])
            pT = PS(KC * 128, BF16)
            TR([(pT.ap[:, k * 128:(k + 1) * 128], xb.ap[:, k * 128:(k + 1) * 128], identb.ap) for k in range(KC)],
               [xb.reg(), identb.reg()], [pT.reg()])
            dv = dstT.ap.rearrange("p (k n) -> p k n", n=ncolsT)[:, :, b * 128:(b + 1) * 128]
            wregs = [dstT.reg(k * ncolsT + b * 128, k * ncolsT + (b + 1) * 128) for k in range(KC)]
            TT("dve", dv, pT.v3(128), bc(gcol.ap, 2, [128, KC, 128]), ALU.mult, [pT.reg(), gcol.reg()], wregs)

        def state_load(src_dram):
            DMA("sp", cacheTok.ap.rearrange("p (j n) -> p j n", n=128),
                src_dram.rearrange("(j q) n -> q j n", q=128), writes=[cacheTok.reg()])
            pS = psA.at(5 * 2048, D, F32)
            TR([(pS.ap[:, j * 128:(j + 1) * 128], cacheTok.ap[:, j * 128:(j + 1) * 128], identf.ap) for j in range(8)],
               [cacheTok.reg(), identf.reg()], [pS.reg()])
            CP("dve", hT.ap, pS.ap, [pS.reg()], [hT.reg()])
            CP("act", hTb.ap, hT.ap, [hT.reg()], [hTb.reg()])

        def state_export(dst_dram):
            pS = psA.at(5 * 2048, D, F32)
            TR([(pS.ap[:, j * 128:(j + 1) * 128], hT.ap[:, j * 128:(j + 1) * 128], identf.ap) for j in range(8)],
               [hT.reg(), identf.reg()], [pS.reg()])
            CP("dve", tailb.ap, pS.ap, [pS.reg()], [tailb.reg()])
            DMA("pool", dst_dram.rearrange("(j q) n -> q j n", q=128),
                tailb.ap.rearrange("p (j n) -> p j n", n=128), reads=[tailb.reg()])

        def ssd_small(ps_dt, mask_ap, mask_reg, nb):
            o_dt = 16
            o_a = 80
            W = nb * NH
            dt_b = small.ap[:, o_dt:o_dt + W]
            a_b = small.ap[:, o_a:o_a + W]
            r_dt = small.reg(o_dt, o_dt + W)
            r_a = small.reg(o_a, o_a + W)
            TT("dve", dt_b.rearrange("p (b h) -> p b h", h=NH), ps_dt.v3(NH), bc(dtb_ap, 1, [128, nb, NH]), ALU.add,
               [ps_dt.reg(), hvec.reg()], [r_dt])
            ACT(dt_b, dt_b, AF.Exp, [r_dt], [r_dt])
            ACT(dt_b, dt_b, AF.Ln, [r_dt, onec.reg()], [r_dt], bias=onec.ap)
            TT("dve", dt_b.rearrange("p (b h) -> p b h", h=NH), dt_b.rearrange("p (b h) -> p b h", h=NH),
               bc(mask_ap, 2, [128, nb, NH]), ALU.mult, [r_dt, mask_reg], [r_dt])
            TT("dve", a_b.rearrange("p (b h) -> p b h", h=NH), dt_b.rearrange("p (b h) -> p b h", h=NH),
               bc(A_ap, 1, [128, nb, NH]), ALU.mult, [r_dt, hvec.reg()], [r_a])
            return o_dt, o_a

        def ssd_decays_all(o_dt, o_a, nb, group_level=False):
            W = nb * NH
            a_all = small.ap[:, o_a:o_a + W]
            r_a = small.reg(o_a, o_a + W)
            pc = PS(2 * W, F32)
            if not group_level:
                MMS([(pc.ap[:, 0:W], triu.ap, a_all), (pc.ap[:, W:2 * W], onesf.ap, a_all)],
                    [triu.reg(), onesf.reg(), r_a], [pc.reg()])
            else:
                MMS([(pc.ap[:, 0:W], triu.ap, a_all)], [triu.reg(), r_a], [pc.reg()])
                for b in range(nb):
                    MM(pc.ap[:, W + b * NH:W + (b + 1) * NH],
                       [(onesf.ap, small.ap[:, o_a + b2 * NH:o_a + (b2 + 1) * NH]) for b2 in range(b, nb)],
                       [onesf.reg(), r_a], [pc.reg()])
            o_ex = 160
            o_w2 = 352
            r_ex = small.reg(o_ex, o_ex + 3 * W)
            CP("act", small.ap[:, o_ex:o_ex + 2 * W], pc.ap, [pc.reg()], [small.reg(o_ex, o_ex + 2 * W)])
            TT("dve", small.ap[:, o_ex + 2 * W:o_ex + 3 * W], small.ap[:, o_ex + W:o_ex + 2 * W], small.ap[:, o_ex:o_ex + W],
               ALU.subtract, [small.reg(o_ex, o_ex + 2 * W)], [small.reg(o_ex + 2 * W, o_ex + 3 * W)])
            if group_level:
                TT("dve", atacc.ap, atacc.ap, small.ap[:, o_ex + W:o_ex + W + NH], ALU.add,
                   [atacc.reg(), small.reg(o_ex + W, o_ex + W + NH)], [atacc.reg()])
            ACT(small.ap[:, o_ex:o_ex + 3 * W], small.ap[:, o_ex:o_ex + 3 * W], AF.Exp, [r_ex], [r_ex])
            TT("dve", small.ap[:, o_w2:o_w2 + W], small.ap[:, o_dt:o_dt + W], small.ap[:, o_ex + 2 * W:o_ex + 3 * W],
               ALU.mult, [small.reg(o_dt, o_dt + W), r_ex], [small.reg(o_w2, o_w2 + W)])

            def sl(o, b):
                return (small.ap[:, o + b * NH:o + (b + 1) * NH], small.reg(o + b * NH, o + (b + 1) * NH))
            return [dict(a=sl(o_a, b), dt=sl(o_dt, b), ea=sl(o_ex, b), eatot=sl(o_ex + W, b), w2=sl(o_w2, b))
                    for b in range(nb)]

        def state_update(dec, b, Btok_blk_ap, Btok_reg, xtok_ap, xtok_reg):
            w2_ap, w2_reg = dec["w2"]
            TT("pool", xdts.ap.rearrange("p (h q) -> p h q", q=HP), xtok_ap.rearrange("p (h q) -> p h q", q=HP),
               bc(w2_ap, 2, [128, NH, HP]), ALU.mult, [xtok_reg, w2_reg], [xdts.reg()])
            pSt = PS(D, F32, nbanks=2)
            MMS([(pSt.ap[:, g * 256:(g + 1) * 256], Btok_blk_ap[:, g * 128:(g + 1) * 128],
                  xdts.ap[:, g * 256:(g + 1) * 256]) for g in range(NG)],
                [Btok_reg, xdts.reg()], [pSt.reg()])
            et_ap, et_reg = dec["eatot"]
            TT("pool", hT.ap.rearrange("p (h q) -> p h q", q=HP), hT.ap.rearrange("p (h q) -> p h q", q=HP),
               bc(et_ap, 2, [128, NH, HP]), ALU.mult, [hT.reg(), et_reg], [hT.reg()])
            TT("dve", hT.ap, hT.ap, pSt.ap, ALU.add, [hT.reg(), pSt.reg()], [hT.reg()])
            CP("act", hTb.ap, hT.ap, [hT.reg()], [hTb.reg()])

        dstate = {"A": 0, "S": 0, "F": 0}

        DG = {"A": (diagA, wA, WA, CA), "S": (diagS, wS, WX, CX), "F": (diagF, wF, WF, CF)}
        pending_builds = [(k, c) for k in ("A", "S", "F") for c in range(DG[k][3])]

        def emit_builds(n):
            for _ in range(n):
                if not pending_builds:
                    return
                kind, c = pending_builds.pop(0)
                bufs, wbuf, W, _n = DG[kind]
                dbuf = bufs[dstate[kind] % 2]
                dstate[kind] += 1
                TT("pool", dbuf.v3(128), bc(identb.ap, 1, [128, W, 128]),
                   bc(wbuf.ap[:, c * W:(c + 1) * W], 2, [128, W, 128]), ALU.mult,
                   [identb.reg(), wbuf.reg()], [dbuf.reg()])
                DMA("pool", dg_dram[kind][c], dbuf.ap, reads=[dbuf.reg()], writes=[dreg("dg" + kind, c, c + 1)])

        def build_diag(kind, c):
            bufs, wbuf, W, _n = DG[kind]
            dbuf = bufs[dstate[kind] % 2]
            dstate[kind] += 1
            DMA("sp", dbuf.ap, dg_dram[kind][c], reads=[dreg("dg" + kind, c, c + 1)], writes=[dbuf.reg()])
            return dbuf

        if NPRE:
            PG = 4
            wpx = sb.at(xbc_ext.off, KC * 1536, BF16)
            wpd = sb.at(t2.off, KC * NH, BF16)
            DMA("sp", wpx.v3(1536), wb_in[:, 3072:4608].rearrange("(k p) n -> p k n", p=128),
                reads=[dreg("wb_inx")], writes=[wpx.reg()])
            DMA("sp", wpd.v3(NH), wb_in[:, 5120:5136].rearrange("(k p) n -> p k n", p=128),
                reads=[dreg("wb_inx")], writes=[wpd.reg()])
            pxe = sb.at(rhs_seg.off, 12 * (512 + HX), BF16)
            assert pxe.off + pxe.n * 2 <= MT.off
            MEMSET("dve", pxh, 0.0)
            pxb = [sb.at(tmpf[0].off, NMAX, BF16), sb.at(tmpf[0].off + NMAX * 2, NMAX, BF16)]
            pdiag = sb.at((wpx.off + wpx.n * 2 + 31) // 32 * 32, 12 * WX * 128, BF16)
            assert pdiag.off + pdiag.n * 2 <= BT.off, "pdiag overflow"
            pdv = pdiag.ap.rearrange("p (c k j) -> p c k j", k=WX, j=128)
            for c in range(12):
                TT("pool", pdv[:, c], bc(identb.ap, 1, [128, WX, 128]),
                   bc(wS.ap[:, c * WX:(c + 1) * WX], 2, [128, WX, 128]), ALU.mult,
                   [identb.reg(), wS.reg()], [pdiag.reg(c * WX * 128, (c + 1) * WX * 128)])
            xdts_all = sb.at(xdt.off, PG * D, BF16)
            assert xdts_all.off + xdts_all.n * 2 <= t2.off
            ngroups_pre = (NPRE + PG - 1) // PG
            builds_per_group = (len(pending_builds) + ngroups_pre - 1) // ngroups_pre
            for g0 in range(0, NPRE, PG):
                nb = min(PG, NPRE - g0)
                N = nb * 128
                DMA("sp", xg.ap[:, 0:nb * D].rearrange("p (b d) -> p b d", d=D),
                    xpre[g0 * 128:(g0 + nb) * 128, :].rearrange("(b p) d -> p b d", p=128),
                    writes=[xg.reg(0, nb * D)])
                norm_transpose_group(nb, g_mix, hnT, NMAX)
                hv = hnT.v3(NMAX)
                pv = pxe.v3(512 + HX)
                CP("pool", pv[:, :, 0:HX], pxh.v3(HX), [pxh.reg()], [pxe.reg()])
                for c in range(12):
                    pp = PS(N, F32)
                    MM(pp.ap, [(wpx.v3(1536)[:, k, c * 128:(c + 1) * 128], hv[:, k, 0:N]) for k in range(KC)],
                       [wpx.reg(), hnT.reg()], [pp.reg()])
                    CP("act" if c % 2 == 0 else "dve", pv[:, c, HX:HX + N], pp.ap, [pp.reg()],
                       [pxe.reg(c * (512 + HX) + HX, c * (512 + HX) + HX + N)])
                CP("pool", pxh.v3(HX), pv[:, :, N:N + HX], [pxe.reg()], [pxh.reg()])
                pdt = PS(nb * NH, F32)
                for b in range(nb):
                    MM(pdt.ap[:, b * NH:(b + 1) * NH],
                       [(hv[:, k, b * 128:(b + 1) * 128], wpd.v3(NH)[:, k, :]) for k in range(KC)],
                       [hnT.reg(), wpd.reg()], [pdt.reg(b * NH, (b + 1) * NH)])
                o_dt, o_a = ssd_small(pdt, pmask.ap[:, g0:g0 + nb], pmask.reg(), nb)
                decs = ssd_decays_all(o_dt, o_a, nb, group_level=True)
                def pre_conv(c):
                    pp = PS(N, F32)
                    MM(pp.ap, [(pdv[:, c, k, :], pv[:, c, k:k + N]) for k in range(WX)],
                       [pdiag.reg(c * WX * 128, (c + 1) * WX * 128), pxe.reg(c * (512 + HX), (c + 1) * (512 + HX))],
                       [pp.reg()])
                    if c < 8:
                        tb_ = pxb[c % 2]
                        ACT(tb_.ap[:, 0:N], pp.ap, AF.Silu, [pp.reg(), bS.reg()], [tb_.reg(0, N)], bias=bS.ap[:, c:c + 1])
                    else:
                        g = c - 8
                        ACT(BT.ap[:, g * NMAX:g * NMAX + N], pp.ap, AF.Silu, [pp.reg(), bS.reg()],
                            [BT.reg(g * NMAX, g * NMAX + N)], bias=bS.ap[:, c:c + 1])

                def pre_tr(c):
                    if c < 8:
                        tb_ = pxb[c % 2]
                        pt = PS(nb * 128, BF16)
                        TR([(pt.ap[:, b * 128:(b + 1) * 128], tb_.ap[:, b * 128:(b + 1) * 128], identb.ap)
                            for b in range(nb)], [tb_.reg(0, N), identb.reg()], [pt.reg()])
                        CP("dve", xtok.ap.rearrange("p (b d) -> p b d", d=D)[:, 0:nb, c * 128:(c + 1) * 128],
                           pt.v3(128), [pt.reg()],
                           [xtok.reg(b * D + c * 128, b * D + (c + 1) * 128) for b in range(nb)])
                    else:
                        g = c - 8
                        pt = PS(nb * 128, BF16)
                        TR([(pt.ap[:, b * 128:(b + 1) * 128], BT.ap[:, g * NMAX + b * 128:g * NMAX + (b + 1) * 128],
                             identb.ap) for b in range(nb)], [BT.reg(g * NMAX, g * NMAX + N), identb.reg()], [pt.reg()])
                        CP("dve", Btok.ap.rearrange("p (b n) -> p b n", n=NG * 128)[:, 0:nb, g * 128:(g + 1) * 128],
                           pt.v3(128), [pt.reg()],
                           [Btok.reg(b * NG * 128 + g * 128, b * NG * 128 + (g + 1) * 128) for b in range(nb)])
                pre_conv(0)
                for c in range(12):
                    if c + 1 < 12:
                        pre_conv(c + 1)
                    pre_tr(c)
                W = nb * NH
                w2_all = small.ap[:, 352:352 + W]
                TT("dve", xdts_all.ap[:, 0:nb * D].rearrange("p (b h q) -> p b h q", h=NH, q=HP),
                   xtok.ap[:, 0:nb * D].rearrange("p (b h q) -> p b h q", h=NH, q=HP),
                   bc(w2_all.rearrange("p (b h) -> p b h", h=NH), 3, [128, nb, NH, HP]), ALU.mult,
                   [xtok.reg(0, nb * D), small.reg(352, 352 + W)], [xdts_all.reg(0, nb * D)])
                pSt = PS(D, F32, nbanks=2)
                for g in range(NG):
                    MM(pSt.ap[:, g * 256:(g + 1) * 256],
                       [(Btok.ap[:, b * NG * 128 + g * 128:b * NG * 128 + (g + 1) * 128],
                         xdts_all.ap[:, b * D + g * 256:b * D + (g + 1) * 256]) for b in range(nb)],
                       [Btok.reg(), xdts_all.reg(0, nb * D)], [pSt.reg()])
                eg_ap, eg_reg = decs[0]["eatot"]
                TT("pool", hT.ap.rearrange("p (h q) -> p h q", q=HP), hT.ap.rearrange("p (h q) -> p h q", q=HP),
                   bc(eg_ap, 2, [128, NH, HP]), ALU.mult, [hT.reg(), eg_reg], [hT.reg()])
                TT("dve", hT.ap, hT.ap, pSt.ap, ALU.add, [hT.reg(), pSt.reg()], [hT.reg()])
                emit_builds(builds_per_group)
                emit_conversions(9)

            DMA("pool", cc_in[0:128, :], hT.ap, reads=[hT.reg()], writes=[dreg("cc_in")])
            S.add("dve", lambda h: h.memset(xtok.ap[0:1, 0:D], 0.0), writes=[xtok.reg(0, D)])
            CP("dve", xtok.ap[0:1, 0:NH], atacc.ap[0:1, :], [atacc.reg()], [xtok.reg(0, D)])
            DMA("pool", cc_in[128:129, :], xtok.ap[0:1, 0:D], reads=[xtok.reg(0, D)], writes=[dreg("cc_in")])
            S.add("pool", lambda h: h.collective_compute("AllGather", ALU.bypass, replica_groups=RGROUPS,
                                                         ins=[cc_in], outs=[cc_out]),
                  reads=[dreg("cc_in")], writes=[dreg("cc_out")], cc=True)
            MEMSET("dve", hT, 0.0)
            for r in range(NCORES):
                DMA("sp", small.ap[:, 440 + r * NH:440 + (r + 1) * NH],
                    cc_out[r * 129 + 128:r * 129 + 129, 0:NH].partition_broadcast(128),
                    reads=[dreg("cc_out")], writes=[small.reg(440 + r * NH, 440 + (r + 1) * NH)])
                DMA("sp", xtok.ap[:, r * D:(r + 1) * D], cc_out[r * 129:r * 129 + 128, :], reads=[dreg("cc_out")],
                    writes=[xtok.reg(r * D, (r + 1) * D)])
            for r in range(NCORES):
                arow = small.ap[:, 440 + r * NH:440 + (r + 1) * NH]
                r_arow = small.reg(440 + r * NH, 440 + (r + 1) * NH)
                TS("dve", arow, arow, cmask.ap[:, r:r + 1], ALU.mult, [r_arow, cmask.reg()], [r_arow])
                ACT(arow, arow, AF.Exp, [r_arow], [r_arow])
                TT("pool", hT.ap.rearrange("p (h q) -> p h q", q=HP), hT.ap.rearrange("p (h q) -> p h q", q=HP),
                   bc(arow, 2, [128, NH, HP]), ALU.mult, [hT.reg(), r_arow], [hT.reg()])
                STT("dve", hT.ap, xtok.ap[:, r * D:(r + 1) * D], cmask.ap[:, r:r + 1], hT.ap, ALU.mult, ALU.add,
                    [xtok.reg(r * D, (r + 1) * D), cmask.reg(), hT.reg()], [hT.reg()])
            DMA("pool", hpre_d, hT.ap, reads=[hT.reg()], writes=[dreg("hpre")])

        emit_conversions(len(conv_q))
        emit_builds(len(pending_builds))

        samp_of = {}
        si = 0
        for i, k in enumerate(kinds):
            if k in ("A", "B"):
                samp_of[i] = si
                si += 1

        def tail_site(blk):
            if kinds[blk] in ("A", "B"):
                return (HA, 16)
            if blk == last_main:
                return (128 - HA, HA)
            return None

        for gi, grp in enumerate(groups):
            nb = len(grp)
            N = nb * 128
            t0 = grp[0]
            assert grp == list(range(t0, t0 + nb))
            xgv = xg.ap[:, 0:nb * D].rearrange("p (b d) -> p b d", d=D)
            for b in range(nb):
                DMA("sp", xg.ap[:, b * D:(b + 1) * D], xs[(t0 + b) * 128:(t0 + b + 1) * 128, :],
                    writes=[xg.reg(b * D, (b + 1) * D)])
            for b in range(nb):
                norm_transpose_block(b, g_mix, hnT, NMAX)
            hv = hnT.v3(NMAX)
            uv = u_ext.v3(NMAX + HA)
            xv = xbc_ext.v3(NMAX + HX)
            UW = NMAX + HA
            XW = NMAX + HX
            CP("pool", uv[:, :, 0:HA], uh.v3(HA), [uh.reg()], [u_ext.reg()])
            CP("pool", xv[:, :, 0:HX], xh.v3(HX), [xh.reg()], [xbc_ext.reg()])

            tails = [(b, tail_site(t0 + b)) for b in range(nb) if tail_site(t0 + b) is not None]

            ag_state = {}

            def ag_a(c):
                half, j = c // 4, c % 4
                if j == 0:
                    ag_state["wa"] = load_piece(wb_in, "wb_in", 0, KC, half * 512, 512)
                    ag_state["wg"] = load_piece(wb_in, "wb_in", 0, KC, 1024 + half * 512, 512)
                wa, ra = ag_state["wa"]
                pa_ = psA.at(2 * 2048, N, F32)
                MM(pa_.ap, [(wa[:, k, j * 128:(j + 1) * 128], hv[:, k, 0:N]) for k in range(KC)],
                   [ra, hnT.reg()], [pa_.reg()])

            def ag_g(c):
                half, j = c // 4, c % 4
                wa, ra = ag_state["wa"]
                wg, rg = ag_state["wg"]
                pa_ = psA.at(2 * 2048, N, F32)
                pg_ = psA.at(7 * 2048, N, F32)
                MM(pg_.ap, [(wg[:, k, j * 128:(j + 1) * 128], hv[:, k, 0:N]) for k in range(KC)],
                   [rg, hnT.reg()], [pg_.reg()])
                tf = tmpf[c % 2]
                ACT(tf.ap[:, 0:N], pg_.ap, AF.Sigmoid, [pg_.reg()], [tf.reg(0, N)])
                TT("dve", uv[:, c, HA:HA + N], pa_.ap, tf.ap[:, 0:N], ALU.mult, [pa_.reg(), tf.reg(0, N)],
                   [u_ext.reg(c * UW + HA, c * UW + HA + N)])
                if j == 3:
                    for (b, (r0, nr)) in tails:
                        pa2 = psA.at(2 * 2048, 512, F32)
                        pg2 = psA.at(7 * 2048, 512, F32)
                        c0 = b * 128 + r0
                        MM(pa2.ap[0:nr, :], [(hv[:, k, c0:c0 + nr], wa[:, k, :]) for k in range(KC)],
                           [ra, hnT.reg()], [pa2.reg()])
                        MM(pg2.ap[0:nr, :], [(hv[:, k, c0:c0 + nr], wg[:, k, :]) for k in range(KC)],
                           [rg, hnT.reg()], [pg2.reg()])
                        tf = tmpf[0]
                        ACT(tf.ap[0:nr, 0:512], pg2.ap[0:nr, :], AF.Sigmoid, [pg2.reg()], [tf.reg(0, 512)])
                        TT("dve", tailb.ap[0:nr, 0:512], pa2.ap[0:nr, :], tf.ap[0:nr, 0:512], ALU.mult,
                           [pa2.reg(), tf.reg(0, 512)], [tailb.reg(0, 512)])
                        blk = t0 + b
                        cs = slice(half * 512, (half + 1) * 512)
                        if kinds[blk] in ("A", "B"):
                            s_ = samp_of[blk]
                            DMA("pool", o_conv_a_s[s_, HA - 16:HA, cs], tailb.ap[0:16, 0:512], reads=[tailb.reg(0, 512)])
                            if half == 0:
                                DMA("pool", o_conv_a_s[s_, 0:HA - 16, :], c_conv_a[s_, 16:HA, :])
                        else:
                            DMA("pool", o_conv_a_p[:, cs], tailb.ap[0:HA, 0:512], reads=[tailb.reg(0, 512)])
            fill_q = []
            for c_ in range(CA):
                fill_q.append((ag_a, c_))
                fill_q.append((ag_g, c_))

            def fill(n=1):
                for _ in range(n):
                    if fill_q:
                        f_, c_ = fill_q.pop(0)
                        f_(c_)
            for half in range(2):
                wz, rz = load_piece(wb_in, "wb_in", 0, KC, 2048 + half * 512, 512)
                for b in range(nb):
                    pz = PS(512, F32)
                    MM(pz.ap, [(hv[:, k, b * 128:(b + 1) * 128], wz[:, k, :]) for k in range(KC)],
                       [rz, hnT.reg()], [pz.reg()])
                    ACT(sz.ap[:, b * D + half * 512:b * D + (half + 1) * 512], pz.ap, AF.Silu, [pz.reg()],
                        [sz.reg(b * D + half * 512, b * D + (half + 1) * 512)])
            for q in range(4):
                wx, rx = load_piece(wb_in, "wb_in", 0, KC, 3072 + q * 512, 512)
                for j in range(4):
                    c = q * 4 + j
                    pp = PS(N, F32)
                    MM(pp.ap, [(wx[:, k, j * 128:(j + 1) * 128], hv[:, k, 0:N]) for k in range(KC)],
                       [rx, hnT.reg()], [pp.reg()])
                    CP("act" if c % 2 == 0 else "dve", xv[:, c, HX:HX + N], pp.ap, [pp.reg()],
                       [xbc_ext.reg(c * XW + HX, c * XW + HX + N)])
                for (b, (r0, nr)) in tails:
                    c0 = b * 128 + r0 + nr - HX
                    pp = PS(512, F32)
                    MM(pp.ap[0:HX, :], [(hv[:, k, c0:c0 + HX], wx[:, k, :]) for k in range(KC)],
                       [rx, hnT.reg()], [pp.reg()])
                    CP("dve", cacheTok.ap[0:HX, 0:512], pp.ap[0:HX, :], [pp.reg()], [cacheTok.reg()])
                    blk = t0 + b
                    dst = o_ssm_s[samp_of[blk]] if kinds[blk] in ("A", "B") else o_ssm_p
                    DMA("pool", dst[:, q * 512:(q + 1) * 512], cacheTok.ap[0:HX, 0:512], reads=[cacheTok.reg()])
            wd, rd = load_piece(wb_in, "wb_in", 0, KC, 5120, NH)
            pdt = PS(nb * NH, F32)
            for b in range(nb):
                MM(pdt.ap[:, b * NH:(b + 1) * NH], [(hv[:, k, b * 128:(b + 1) * 128], wd[:, k, :]) for k in range(KC)],
                   [hnT.reg(), rd], [pdt.reg(b * NH, (b + 1) * NH)])
            o_dt, o_a = ssd_small(pdt, tmask.ap[:, t0:t0 + nb], tmask.reg(), nb)
            for b in range(nb):
                blk = t0 + b
                if kinds[blk] in ("A", "B"):
                    s = samp_of[blk]
                    for hh in range(2):
                        DMA("sp", cacheTok.ap[0:HX, :], c_ssm[s, :, hh * D:(hh + 1) * D], writes=[cacheTok.reg()])
                        pi = PS(8 * HX, F32)
                        TR([(pi.ap[:, c * HX:(c + 1) * HX], cacheTok.ap[0:HX, c * 128:(c + 1) * 128],
                             identf.ap[0:HX, 0:HX]) for c in range(8)], [cacheTok.reg(), identf.reg()], [pi.reg()])
                        CP("dve", xv[:, hh * 8:(hh + 1) * 8, HX + b * 128 + HA - HX:HX + b * 128 + HA], pi.v3(HX),
                           [pi.reg()], [xbc_ext.reg()])
            CP("pool", xh.v3(HX), xv[:, :, N:N + HX], [xbc_ext.reg()], [xh.reg()])

            def sconv(c):
                dg = build_diag("S", c)
                pp = PS(N, F32)
                MM(pp.ap, [(dg.v3(128)[:, k, :], xv[:, c, k:k + N]) for k in range(WX)],
                   [dg.reg(), xbc_ext.reg(c * XW, (c + 1) * XW)], [pp.reg()])
                if c < 8:
                    tf = tmpf[c % 2]
                    ACT(tf.ap[:, 0:N], pp.ap, AF.Silu, [pp.reg(), bS.reg()], [tf.reg(0, N)], bias=bS.ap[:, c:c + 1])
                elif c < 12:
                    g = c - 8
                    ACT(BT.ap[:, g * NMAX:g * NMAX + N], pp.ap, AF.Silu, [pp.reg(), bS.reg()],
                        [BT.reg(g * NMAX, g * NMAX + N)], bias=bS.ap[:, c:c + 1])
                else:
                    g = c - 12
                    ACT(CT.ap[:, g * NMAX:g * NMAX + N], pp.ap, AF.Silu, [pp.reg(), bS.reg()],
                        [CT.reg(g * NMAX, g * NMAX + N)], bias=bS.ap[:, c:c + 1])

            def strans(c):
                if c < 8:
                    tf = tmpf[c % 2]
                    pt = PS(nb * 128, F32)
                    TR([(pt.ap[:, b * 128:(b + 1) * 128], tf.ap[:, b * 128:(b + 1) * 128], identf.ap)
                        for b in range(nb)], [tf.reg(0, N), identf.reg()], [pt.reg()])
                    CP("dve", xtok.ap.rearrange("p (b d) -> p b d", d=D)[:, 0:nb, c * 128:(c + 1) * 128],
                       pt.v3(128), [pt.reg()],
                       [xtok.reg(b * D + c * 128, b * D + (c + 1) * 128) for b in range(nb)])
                elif c < 12:
                    g = c - 8
                    pt = PS(nb * 128, BF16)
                    TR([(pt.ap[:, b * 128:(b + 1) * 128], BT.ap[:, g * NMAX + b * 128:g * NMAX + (b + 1) * 128],
                         identb.ap) for b in range(nb)], [BT.reg(g * NMAX, g * NMAX + N), identb.reg()], [pt.reg()])
                    CP("dve", Btok.ap.rearrange("p (b n) -> p b n", n=NG * 128)[:, 0:nb, g * 128:(g + 1) * 128],
                       pt.v3(128), [pt.reg()],
                       [Btok.reg(b * NG * 128 + g * 128, b * NG * 128 + (g + 1) * 128) for b in range(nb)])
            sconv(0)
            for c in range(CX):
                if c + 1 < CX:
                    sconv(c + 1)
                strans(c)
            decs = ssd_decays_all(o_dt, o_a, nb)

            obv = outb.v3(NMAX)

            def PSF(bank, nelem, dtype=F32):
                return psA.at(bank * 2048, nelem, dtype)
            t1s = [t1, t1b]

            def ssd_front(b):
                dec = decs[b]
                a_ap, a_reg = dec["a"]
                dt_ap, dt_reg = dec["dt"]
                xt_ap = xtok.ap[:, b * D:(b + 1) * D]
                xt_reg = xtok.reg(b * D, (b + 1) * D)
                TT("dve", xdt.ap.rearrange("p (h q) -> p h q", q=HP), xt_ap.rearrange("p (h q) -> p h q", q=HP),
                   bc(dt_ap, 2, [128, NH, HP]), ALU.mult, [xt_reg, dt_reg], [xdt.reg()])
                TT("pool", rhs_seg.v3(128), bc(triu.ap, 1, [128, NH, 128]), bc(a_ap, 2, [128, NH, 128]), ALU.mult,
                   [triu.reg(), a_reg], [rhs_seg.reg()])
                pcb = PSF(0, NG * 128)
                MMS([(pcb.ap[:, g * 128:(g + 1) * 128], BT.ap[:, g * NMAX + b * 128:g * NMAX + (b + 1) * 128],
                      CT.ap[:, g * NMAX + b * 128:g * NMAX + (b + 1) * 128]) for g in range(NG)],
                    [BT.reg(), CT.reg()], [pcb.reg()])
                TT("dve", cbTm.v3(128), pcb.v3(128), bc(triu.ap, 1, [128, NG, 128]), ALU.mult,
                   [pcb.reg(), triu.reg()], [cbTm.reg()])
                fill()
                for q in range(4):
                    pseg = PSF(1, 512)
                    MM(pseg.ap, [(lmat.ap, rhs_seg.ap[:, q * 512:(q + 1) * 512])],
                       [lmat.reg(), rhs_seg.reg(q * 512, (q + 1) * 512)], [pseg.reg()])
                    ACT(Emat.ap[:, q * 512:(q + 1) * 512], pseg.ap, AF.Exp, [pseg.reg()], [Emat.reg(q * 512, (q + 1) * 512)])
                TT("dve", MT.v4(4, 128), Emat.v4(4, 128), bc(cbTm.v3(128), 2, [128, NG, 4, 128]), ALU.mult,
                   [Emat.reg(), cbTm.reg()], [MT.reg()])
                fill()
                py = PSF(3, D)
                MMS([(py.ap[:, h * HP:(h + 1) * HP], MT.ap[:, h * 128:(h + 1) * 128], xdt.ap[:, h * HP:(h + 1) * HP])
                     for h in range(NH)], [MT.reg(), xdt.reg()], [py.reg()])

            def ssd_mid(b):
                blk = t0 + b
                dec = decs[b]
                tb = t1s[b % 2]
                if kinds[blk] in ("A", "B"):
                    state_load(c_state[samp_of[blk]])
                fill()
                po = PSF(5, D)
                py = PSF(3, D)
                MMS([(po.ap[:, g * 256:(g + 1) * 256], CT.ap[:, g * NMAX + b * 128:g * NMAX + (b + 1) * 128],
                      hTb.ap[:, g * 256:(g + 1) * 256]) for g in range(NG)], [CT.reg(), hTb.reg()], [po.reg()])
                ea_ap, ea_reg = dec["ea"]
                TT("dve", tb.ap.rearrange("p (h q) -> p h q", q=HP), po.ap.rearrange("p (h q) -> p h q", q=HP),
                   bc(ea_ap, 2, [128, NH, HP]), ALU.mult, [po.reg(), ea_reg], [tb.reg()])
                TT("dve", tb.ap, py.ap, tb.ap, ALU.add, [py.reg(), tb.reg()], [tb.reg()])
                xt_ap = xtok.ap[:, b * D:(b + 1) * D]
                xt_reg = xtok.reg(b * D, (b + 1) * D)
                w2_ap, w2_reg = dec["w2"]
                TT("pool", xdts.ap.rearrange("p (h q) -> p h q", q=HP), xt_ap.rearrange("p (h q) -> p h q", q=HP),
                   bc(w2_ap, 2, [128, NH, HP]), ALU.mult, [xt_reg, w2_reg], [xdts.reg()])
                pSt = PSF(5, D)
                MMS([(pSt.ap[:, g * 256:(g + 1) * 256], Btok.ap[:, b * NG * 128 + g * 128:b * NG * 128 + (g + 1) * 128],
                      xdts.ap[:, g * 256:(g + 1) * 256]) for g in range(NG)],
                    [Btok.reg(b * NG * 128, (b + 1) * NG * 128), xdts.reg()], [pSt.reg()])
                et_ap, et_reg = dec["eatot"]
                TT("pool", hT.ap.rearrange("p (h q) -> p h q", q=HP), hT.ap.rearrange("p (h q) -> p h q", q=HP),
                   bc(et_ap, 2, [128, NH, HP]), ALU.mult, [hT.reg(), et_reg], [hT.reg()])
                TT("dve", hT.ap, hT.ap, pSt.ap, ALU.add, [hT.reg(), pSt.reg()], [hT.reg()])
                CP("act", hTb.ap, hT.ap, [hT.reg()], [hTb.reg()])
                if kinds[blk] in ("A", "B"):
                    state_export(o_state_s[samp_of[blk]])
                    if kinds[blk + 1] not in ("A", "B"):
                        if NPRE:
                            DMA("sp", hT.ap, hpre_d, reads=[dreg("hpre")], writes=[hT.reg()])
                        else:
                            MEMSET("dve", hT, 0.0)
                        CP("act", hTb.ap, hT.ap, [hT.reg()], [hTb.reg()])
                if blk == last_main:
                    state_export(o_state_p)

            def ssd_back(b):
                tb = t1s[b % 2]
                xt_ap = xtok.ap[:, b * D:(b + 1) * D]
                xt_reg = xtok.reg(b * D, (b + 1) * D)
                TT("pool", t2.ap.rearrange("p (h q) -> p h q", q=HP), xt_ap.rearrange("p (h q) -> p h q", q=HP),
                   bc(Dsk_ap, 2, [128, NH, HP]), ALU.mult, [xt_reg, hvec.reg()], [t2.reg()])
                TT("pool", tb.ap, tb.ap, t2.ap, ALU.add, [tb.reg(), t2.reg()], [tb.reg()])
                TT("dve", tb.ap, tb.ap, sz.ap[:, b * D:(b + 1) * D], ALU.mult, [tb.reg(), sz.reg(b * D, (b + 1) * D)],
                   [tb.reg()])
                ms = small.ap[:, 12:13]
                ACT(ygn.ap, tb.ap, AF.Square, [tb.reg()], [ygn.reg(), small.reg(12, 13)], scale=1.0 / 32.0, accum=ms)
                ACT(small.ap[:, 13:14], ms, AF.Ln, [small.reg(12, 13), epsc.reg()], [small.reg(13, 14)], bias=epsc.ap)
                ACT(small.ap[:, 14:15], small.ap[:, 13:14], AF.Exp, [small.reg(13, 14)], [small.reg(14, 15)], scale=-0.5)
                TS("dve", ygn.ap, tb.ap, small.ap[:, 14:15], ALU.mult, [tb.reg(), small.reg(14, 15)], [ygn.reg()])
                fill()
                pT = PSF(0, KC * 128, BF16)
                TR([(pT.ap[:, k * 128:(k + 1) * 128], ygn.ap[:, k * 128:(k + 1) * 128], identb.ap) for k in range(KC)],
                   [ygn.reg(), identb.reg()], [pT.reg()])
                TT("dve", obv[:, :, b * 128:(b + 1) * 128], pT.v3(128), bc(g_ssm.ap, 2, [128, KC, 128]), ALU.mult,
                   [pT.reg(), g_ssm.reg()], [outb.reg(k * NMAX + b * 128, k * NMAX + (b + 1) * 128) for k in range(KC)])

            ssd_front(0)
            for b in range(nb):
                ssd_mid(b)
                if b + 1 < nb:
                    ssd_front(b + 1)
                ssd_back(b)
            fill(len(fill_q))
            for b in range(nb):
                blk = t0 + b
                if kinds[blk] in ("A", "B"):
                    s_ = samp_of[blk]
                    DMA("sp", cacheTok.ap[0:HA, :], c_conv_a[s_], writes=[cacheTok.reg()])
                    pi = PS(CA * HA, F32)
                    TR([(pi.ap[:, c * HA:(c + 1) * HA], cacheTok.ap[0:HA, c * 128:(c + 1) * 128], identf.ap[0:HA, 0:HA])
                        for c in range(CA)], [cacheTok.reg(), identf.reg()], [pi.reg()])
                    CP("dve", uv[:, :, HA + b * 128:HA + b * 128 + HA], pi.v3(HA), [pi.reg()], [u_ext.reg()])
            CP("pool", uh.v3(HA), uv[:, :, N:N + HA], [u_ext.reg()], [uh.reg()])

            ucfv = ucf.v3(NMAX)
            ucbv = ucb.v3(NMAX)
            sqbv = sqb.v3(NMAX)
            for c in range(CA):
                dg = build_diag("A", c)
                pp = PS(N, F32)
                MM(pp.ap, [(dg.v3(128)[:, k, :], uv[:, c, k:k + N]) for k in range(WA)],
                   [dg.reg(), u_ext.reg(c * UW, (c + 1) * UW)], [pp.reg()])
                rr = ucf.reg(c * NMAX, c * NMAX + N)
                ACT(ucfv[:, c, 0:N], pp.ap, AF.Identity, [pp.reg(), bA.reg()], [rr], bias=bA.ap[:, c:c + 1])
                CP("pool", ucbv[:, c, 0:N], ucfv[:, c, 0:N], [rr], [ucb.reg(c * NMAX, c * NMAX + N)])
                TT("dve", sqbv[:, c, 0:N], ucfv[:, c, 0:N], ucfv[:, c, 0:N], ALU.mult, [rr],
                   [sqb.reg(c * NMAX, c * NMAX + N)])
            p1 = PS(N, F32)
            p2 = PS(N, F32)
            MM(p1.ap, [(onesb.ap, ucbv[:, c, 0:N]) for c in range(CA)], [onesb.reg(), ucb.reg()], [p1.reg()])
            MM(p2.ap, [(onesb.ap, sqbv[:, c, 0:N]) for c in range(CA)], [onesb.reg(), sqb.reg()], [p2.reg()])
            ACT(mu.ap[:, 0:N], p1.ap, AF.Copy, [p1.reg()], [mu.reg(0, N)], scale=1.0 / D)
            TT("dve", rstdN.ap[:, 0:N], mu.ap[:, 0:N], mu.ap[:, 0:N], ALU.mult, [mu.reg(0, N)], [rstdN.reg(0, N)])
            STT("dve", rstdN.ap[:, 0:N], p2.ap, 1.0 / D, rstdN.ap[:, 0:N], ALU.mult, ALU.subtract,
                [p2.reg(), rstdN.reg(0, N)], [rstdN.reg(0, N)])
            ACT(rstdN.ap[:, 0:N], rstdN.ap[:, 0:N], AF.Ln, [rstdN.reg(0, N), epsc.reg()], [rstdN.reg(0, N)], bias=epsc.ap)
            ACT(rstdN.ap[:, 0:N], rstdN.ap[:, 0:N], AF.Exp, [rstdN.reg(0, N)], [rstdN.reg(0, N)], scale=-0.5)
            oav = outa.v3(NMAX)

            def ln_apply(c):
                rr = ucf.reg(c * NMAX, c * NMAX + N)
                TT("dve", ucfv[:, c, 0:N], ucfv[:, c, 0:N], mu.ap[:, 0:N], ALU.subtract, [rr, mu.reg(0, N)], [rr])
                TT("pool", ucfv[:, c, 0:N], ucfv[:, c, 0:N], rstdN.ap[:, 0:N], ALU.mult, [rr, rstdN.reg(0, N)], [rr])
                ACT(oav[:, c, 0:N], ucfv[:, c, 0:N], AF.Silu, [rr, lng.reg(), lnb.reg()],
                    [outa.reg(c * NMAX, c * NMAX + N)], bias=lnb.ap[:, c:c + 1], scale=lng.ap[:, c:c + 1])

            for c in range(CA):
                ln_apply(c)

            obanks = [[PS(512, F32) for _ in range(nb)] for _ in range(2)]
            for half in range(2):
                wo2, ro2 = load_piece(wb_out, "wb_out", D, KC, half * 512, 512)
                for b in range(nb):
                    MM(obanks[half][b].ap, [(obv[:, k, b * 128:(b + 1) * 128], wo2[:, k, :]) for k in range(KC)],
                       [ro2, outb.reg()], [obanks[half][b].reg()], start=True, stop=False)
            hfT = R1
            for half in range(2):
                wo, ro = load_piece(wb_out, "wb_out", 0, KC, half * 512, 512)
                for b in range(nb):
                    MM(obanks[half][b].ap, [(oav[:, k, b * 128:(b + 1) * 128], wo[:, k, :]) for k in range(KC)],
                       [ro] + [outa.reg(k * NMAX + b * 128, k * NMAX + (b + 1) * 128) for k in range(KC)],
                       [obanks[half][b].reg()], start=False, stop=True)
                    xr = xg.reg(b * D + half * 512, b * D + (half + 1) * 512)
                    xa = xg.ap[:, b * D + half * 512:b * D + (half + 1) * 512]
                    TT("dve", xa, xa, obanks[half][b].ap, ALU.add, [xr, obanks[half][b].reg()], [xr])
                    if half == 1:
                        norm_transpose_block(b, g_ffn, hfT, NMAX)

            gv = gate_ext.v3(NMAX + HG)
            GW = NMAX + HG
            av = actT.v3(NMAX)
            CP("pool", gv[:, :, 0:HG], gh.v3(HG), [gh.reg()], [gate_ext.reg()])
            npc = (DFF + 511) // 512
            for pi_ in range(npc):
                ncols = min(512, DFF - pi_ * 512)
                wg, rg = load_piece(wb_up, "wb_up", 0, KC, pi_ * 512, ncols)
                for j in range(ncols // 128):
                    c = pi_ * 4 + j
                    pp = PS(N, F32)
                    MM(pp.ap, [(wg[:, k, j * 128:(j + 1) * 128], hv[:, k, 0:N]) for k in range(KC)],
                       [rg, hfT.reg()], [pp.reg()])
                    CP("act" if c % 2 == 0 else "dve", gv[:, c, HG:HG + N], pp.ap, [pp.reg()],
                       [gate_ext.reg(c * GW + HG, c * GW + HG + N)])
                for (b, (r0, nr)) in tails:
                    c0 = b * 128 + r0 + nr - HG
                    pp = PS(512, F32)
                    MM(pp.ap[0:HG, 0:ncols], [(hv[:, k, c0:c0 + HG], wg[:, k, :]) for k in range(KC)],
                       [rg, hfT.reg()], [pp.reg()])
                    CP("dve", cacheTok.ap[0:HG, 0:ncols], pp.ap[0:HG, 0:ncols], [pp.reg()], [cacheTok.reg()])
                    blk = t0 + b
                    dst = o_ffn_s[samp_of[blk]] if kinds[blk] in ("A", "B") else o_ffn_p
                    DMA("pool", dst[:, pi_ * 512:pi_ * 512 + ncols], cacheTok.ap[0:HG, 0:ncols], reads=[cacheTok.reg()])
            for b in range(nb):
                blk = t0 + b
                if kinds[blk] in ("A", "B"):
                    s = samp_of[blk]
                    for (cs, ce) in ((0, 8), (8, 16), (16, 22)):
                        ncx = ce - cs
                        DMA("sp", cacheTok.ap[0:HG, 0:ncx * 128], c_ffn[s, :, cs * 128:ce * 128], writes=[cacheTok.reg()])
                        pi = PS(8 * HG, F32)
                        TR([(pi.ap[:, c * HG:(c + 1) * HG], cacheTok.ap[0:HG, c * 128:(c + 1) * 128],
                             identf.ap[0:HG, 0:HG]) for c in range(ncx)], [cacheTok.reg(), identf.reg()], [pi.reg()])
                        CP("dve", gv[:, cs:ce, HG + b * 128 + HA - HG:HG + b * 128 + HA], pi.v3(HG)[:, 0:ncx, :],
                           [pi.reg()], [gate_ext.reg()])
                if kinds[blk] == "H":
                    sl = gv[:, :, HG + (b + 1) * 128 - HG:HG + (b + 1) * 128]
                    TS("dve", sl, sl, gkeep.ap[:, 0:1], ALU.mult, [gate_ext.reg(), gkeep.reg()], [gate_ext.reg()])
            CP("pool", gh.v3(HG), gv[:, :, N:N + HG], [gate_ext.reg()], [gh.reg()])
            for pi_ in range(npc):
                ncols = min(512, DFF - pi_ * 512)
                wv, rv = load_piece(wb_up, "wb_up", 0, KC, DFF + pi_ * 512, ncols)
                for j in range(ncols // 128):
                    c = pi_ * 4 + j
                    dg = build_diag("F", c)
                    pc_ = PS(N, F32)
                    MM(pc_.ap, [(dg.v3(128)[:, k, :], gv[:, c, k:k + N]) for k in range(WF)],
                       [dg.reg(), gate_ext.reg(c * GW, (c + 1) * GW)], [pc_.reg()])
                    tf = tmpf[c % 2]
                    ACT(tf.ap[:, 0:N], pc_.ap, AF.Silu, [pc_.reg(), bF.reg()], [tf.reg(0, N)], bias=bF.ap[:, c:c + 1])
                    pv_ = PS(N, F32)
                    MM(pv_.ap, [(wv[:, k, j * 128:(j + 1) * 128], hv[:, k, 0:N]) for k in range(KC)],
                       [rv, hfT.reg()], [pv_.reg()])
                    TT("dve", av[:, c, 0:N], pv_.ap, tf.ap[:, 0:N], ALU.mult, [pv_.reg(), tf.reg(0, N)],
                       [actT.reg(c * NMAX, c * NMAX + N)])
            def final_block(b):
                o = 416 + 3 * b
                xa = xg.ap[:, b * D:(b + 1) * D]
                xr = xg.reg(b * D, (b + 1) * D)
                ya = yst.ap[:, b * D:(b + 1) * D]
                yr = yst.reg(b * D, (b + 1) * D)
                ACT(MT.ap[:, 0:D], xa, AF.Square, [xr], [MT.reg(0, D), small.reg(o, o + 1)], scale=1.0 / 32.0,
                    accum=small.ap[:, o:o + 1])
                ACT(small.ap[:, o + 1:o + 2], small.ap[:, o:o + 1], AF.Ln, [small.reg(o, o + 1), epsc.reg()],
                    [small.reg(o + 1, o + 2)], bias=epsc.ap)
                ACT(small.ap[:, o + 2:o + 3], small.ap[:, o + 1:o + 2], AF.Exp, [small.reg(o + 1, o + 2)],
                    [small.reg(o + 2, o + 3)], scale=-0.5)
                STT("dve", ya, xa, small.ap[:, o + 2:o + 3], nfin.ap, ALU.mult, ALU.mult,
                    [xr, small.reg(o + 2, o + 3), nfin.reg()], [yr])
                DMA("pool", y_d[(t0 + b) * 128:(t0 + b + 1) * 128, :], ya, reads=[yr])

            kgs = ((0, 8), (8, 8), (16, 6))
            for half in range(2):
                banks = [PS(512, F32) for _ in range(nb)]
                for gi_, (k0, nk) in enumerate(kgs):
                    wdn, rdn = load_piece(wb_down, "wb_down", k0 * 128, nk, half * 512, 512)
                    for b in range(nb):
                        MM(banks[b].ap, [(av[:, k0 + k, b * 128:(b + 1) * 128], wdn[:, k, :]) for k in range(nk)],
                           [rdn, actT.reg()], [banks[b].reg()], start=(gi_ == 0), stop=(gi_ == len(kgs) - 1))
                        if gi_ == len(kgs) - 1:
                            xr = xg.reg(b * D + half * 512, b * D + (half + 1) * 512)
                            xa = xg.ap[:, b * D + half * 512:b * D + (half + 1) * 512]
                            TT("dve", xa, xa, banks[b].ap, ALU.add, [xr, banks[b].reg()], [xr])
                            if half == 1:
                                final_block(b)

        S.emit()
        build_nc.last_stats = S.stats
    return nc


def _cfg(segtok, npre_blocks):
    nmain = segtok // 128
    kinds = ["A", "B", "H"] + ["M"] * nmain
    groups = [[0, 1, 2]]
    i = 3
    while i < len(kinds):
        n = min(4, len(kinds) - i)
        groups.append(list(range(i, i + n)))
        i += n
    return dict(blocks=kinds, groups=groups, npre=npre_blocks)


def _colvec(v, nchunk):
    v = np.asarray(v, np.float32).reshape(nchunk, 128)
    return np.ascontiguousarray(v.T)


def _convw(w, nchunk):
    W = w.shape[0]
    a = np.asarray(w, np.float32).reshape(W, nchunk, 128)
    return np.ascontiguousarray(a.transpose(2, 1, 0).reshape(128, nchunk * W))


def run_step(inp, seq_len, nseg):
    xp = np.asarray(inp["x_prompt"], np.float32)
    xsamp = np.asarray(inp["x_sample"], np.float32)
    nb_p = xp.shape[0]
    n_s = xsamp.shape[0]
    dec_seq = xsamp.shape[1]
    n_cores = nb_p * nseg
    assert n_s == 2 * n_cores and dec_seq == 16
    segtok = seq_len // nseg
    npre = segtok // 128 + 1
    cfg = _cfg(segtok, npre)
    cfg["ncores"] = n_cores
    cfg["nseg"] = nseg
    nc = build_nc(cfg)
    kinds = cfg["blocks"]
    NBLK = len(kinds)

    common = dict(
        w_in=np.ascontiguousarray(inp["w_in"][0], np.float32),
        w_out=np.ascontiguousarray(inp["w_out"][0], np.float32),
        w_up=np.ascontiguousarray(inp["w_up"][0], np.float32),
        w_down=np.ascontiguousarray(inp["w_down"][0], np.float32),
        norm_mix=_colvec(inp["norm_mix"][0], KC),
        norm_ffn=_colvec(inp["norm_ffn"][0], KC),
        ssm_norm=_colvec(inp["ssm_norm"][0], KC),
        norm_final=np.asarray(inp["norm_final"], np.float32).reshape(1, D),
        wA=_convw(inp["conv_a_w"][0], CA), bA=_colvec(inp["conv_a_b"][0], CA),
        lng=_colvec(inp["ln_a_g"][0], CA), lnb=_colvec(inp["ln_a_b"][0], CA),
        wS=_convw(inp["ssm_conv_w"][0], CX), bS=_colvec(inp["ssm_conv_b"][0], CX),
        wF=_convw(inp["ffn_conv_w"][0], CF), bF=_colvec(inp["ffn_conv_b"][0], CF),
        hvec=np.concatenate([np.asarray(inp["dt_bias"][0], np.float32), np.asarray(inp["a_log"][0], np.float32),
                             np.asarray(inp["d_skip"][0], np.float32)]).reshape(1, 3 * NH),
        ident=np.eye(128, dtype=np.float32),
        triu=np.triu(np.ones((128, 128), np.float32)),
        lmat=np.tril(np.ones((128, 128), np.float32), -1),
    )
    in_maps = []
    for c in range(n_cores):
        seq, k = c // nseg, c % nseg
        start = k * segtok
        xs = np.zeros((NBLK * 128, D), np.float32)
        tmask = np.zeros((128, NBLK), np.float32)
        for i in range(2):
            xs[i * 128 + HA:i * 128 + HA + 16] = xsamp[2 * c + i]
            tmask[HA:HA + 16, i] = 1.0
        if k > 0:
            xs[2 * 128:3 * 128] = xp[seq, start - 128:start]
            tmask[126:128, 2] = 1.0
        xs[3 * 128:] = xp[seq, start:start + segtok]
        tmask[:, 3:] = 1.0
        m = dict(common)
        m.update(xs=xs, tmask=tmask, gkeep=np.full((128, 1), 1.0 if k > 0 else 0.0, np.float32),
                 c_conv_a=np.ascontiguousarray(inp["cache_conv_a"][0, 2 * c:2 * c + 2], np.float32),
                 c_ssm=np.ascontiguousarray(inp["cache_ssm_conv"][0, 2 * c:2 * c + 2], np.float32),
                 c_state=np.ascontiguousarray(inp["state_ssm"][0, 2 * c:2 * c + 2], np.float32).reshape(2, NH * HP, 128),
                 c_ffn=np.ascontiguousarray(inp["cache_ffn_conv"][0, 2 * c:2 * c + 2], np.float32))
        pm = np.zeros((128, npre), np.float32)
        if k > 0:
            pm[126:128, 0] = 1.0
        pm[:, 1:] = 1.0
        pm[126:128, npre - 1] = 0.0
        m["pmask"] = pm
        cm = np.zeros((128, nseg), np.float32)
        cm[:, :k] = 1.0
        m["cmask"] = cm
        in_maps.append(m)
    res = run_bass_kernel_spmd(nc, in_maps, core_ids=list(range(n_cores)))
    R = res.results
    y_prompt = np.zeros((nb_p, seq_len, D), np.float32)
    y_sample = np.zeros((n_s, 16, D), np.float32)
    ca_p = np.zeros((1, nb_p, HA, D), np.float32)
    sc_p = np.zeros((1, nb_p, HX, 2048), np.float32)
    st_p = np.zeros((1, nb_p, NH, HP, 128), np.float32)
    fc_p = np.zeros((1, nb_p, HG, DFF), np.float32)
    ca_s = np.zeros((1, n_s, HA, D), np.float32)
    sc_s = np.zeros((1, n_s, HX, 2048), np.float32)
    st_s = np.zeros((1, n_s, NH, HP, 128), np.float32)
    fc_s = np.zeros((1, n_s, HG, DFF), np.float32)
    for c in range(n_cores):
        seq, k = c // nseg, c % nseg
        start = k * segtok
        r = R[c]
        y = np.asarray(r["y"])
        y_prompt[seq, start:start + segtok] = y[3 * 128:]
        for i in range(2):
            y_sample[2 * c + i] = y[i * 128 + HA:i * 128 + HA + 16]
        ca_s[0, 2 * c:2 * c + 2] = np.asarray(r["o_conv_a_s"])
        sc_s[0, 2 * c:2 * c + 2] = np.asarray(r["o_ssm_s"])
        st_s[0, 2 * c:2 * c + 2] = np.asarray(r["o_state_s"]).reshape(2, NH, HP, 128)
        fc_s[0, 2 * c:2 * c + 2] = np.asarray(r["o_ffn_s"])
        if k == nseg - 1:
            ca_p[0, seq] = np.asarray(r["o_conv_a_p"])
            sc_p[0, seq] = np.asarray(r["o_ssm_p"])
            st_p[0, seq] = np.asarray(r["o_state_p"]).reshape(NH, HP, 128)
            fc_p[0, seq] = np.asarray(r["o_ffn_p"])
    return (y_prompt, y_sample, ca_p, sc_p, st_p, fc_p, ca_s, sc_s, st_s, fc_s)


def kernel(**inputs):
    inp = {k: np.asarray(v) for k, v in inputs.items()}
    return run_step(inp, SEQ, NSEG)
```

```python
import contextlib
import numpy as np
import concourse.bass as bass
import concourse.mybir as mybir
from concourse.bass_utils import run_bass_kernel_spmd

F32 = mybir.dt.float32
BF16 = mybir.dt.bfloat16
AF = mybir.ActivationFunctionType
ALU = mybir.AluOpType

ENGS = ("pe", "act", "dve", "pool", "sp")
N_DMA_SEMS = 8

D = 1024
DIN = 5136
DFF = 2816
KC = 8
CA = 8
CX = 16
CF = 22
NH = 16
HP = 64
NG = 4
HA, HX, HG = 30, 3, 2
WA, WX, WF = 31, 4, 3
EPS = 1e-6
SEQ = 16384
NSEG = 4
SEGTOK = SEQ // NSEG


class Reg:
    __slots__ = ("space", "a", "b")

    def __init__(self, space, a, b):
        self.space, self.a, self.b = space, a, b


class Op:
    __slots__ = ("eng", "fn", "is_dma", "deps", "idx", "ms", "dsem", "dcount", "needed", "name", "cc")


class Sched:
    def __init__(self, nc, same_engine_sync=True):
        self.nc = nc
        self.ops = []
        self.spaces = {}
        self.same_engine_sync = same_engine_sync
        self.nosync_engines = ()

    def _touch(self, op, reg, write):
        ivs = self.spaces.setdefault(reg.space, [])
        a, b = reg.a, reg.b
        if reg.space == "ps":
            a = a // 2048 * 2048
            b = (b + 2047) // 2048 * 2048
            write = True
        new = []
        for iv in ivs:
            if iv[1] <= a or iv[0] >= b:
                new.append(iv)
                continue
            if iv[2] is not None:
                op.deps.add(iv[2])
            if write:
                for r in iv[3]:
                    op.deps.add(r)
                if iv[0] < a:
                    new.append([iv[0], a, iv[2], list(iv[3])])
                if iv[1] > b:
                    new.append([b, iv[1], iv[2], list(iv[3])])
            else:
                iv[3].append(op)
                new.append(iv)
        if write:
            new.append([a, b, op, []])
        else:
            covered = sorted((max(iv[0], a), min(iv[1], b)) for iv in new
                             if not (iv[1] <= a or iv[0] >= b))
            cur = a
            for (s, e) in covered:
                if s > cur:
                    new.append([cur, s, None, [op]])
                cur = max(cur, e)
            if cur < b:
                new.append([cur, b, None, [op]])
        self.spaces[reg.space] = new

    def add(self, eng, fn, reads=(), writes=(), dma=False, name="", cc=False):
        op = Op()
        op.eng, op.fn, op.is_dma, op.name = eng, fn, (dma or cc), name
        op.cc = cc
        op.deps = set()
        op.idx = len(self.ops)
        op.ms = None
        op.needed = False
        for r in reads:
            self._touch(op, r, False)
        for w in writes:
            self._touch(op, w, True)
        if cc:
            self._touch(op, Reg("dma_barrier", 0, 1), True)
        elif dma:
            self._touch(op, Reg("dma_barrier", 0, 1), False)
        op.deps.discard(op)
        self.ops.append(op)
        return op

    def emit(self):
        nc = self.nc
        streams = {e: [] for e in ENGS}
        for op in self.ops:
            streams[op.eng].append(op)
        dma_cnt = {e: 0 for e in ENGS}
        dma_prev = {}
        n_cc = 0
        for op in self.ops:
            if op.is_dma and op.cc:
                prev = dma_prev.get(("cc", 0))
                op.dsem = ("cc", 0)
                op.dcount = (prev.dcount if prev is not None else 0) + 1
                if prev is not None:
                    op.deps.add(prev)
                dma_prev[("cc", 0)] = op
                n_cc += 1
            elif op.is_dma:
                slot = dma_cnt[op.eng] % N_DMA_SEMS
                dma_cnt[op.eng] += 1
                prev = dma_prev.get((op.eng, slot))
                op.dsem = (op.eng, slot)
                op.dcount = (prev.dcount if prev is not None else 0) + 16
                if prev is not None:
                    op.deps.add(prev)
                dma_prev[(op.eng, slot)] = op

        def skip(d, op):
            if d.is_dma:
                return False
            if op.is_dma:
                return False
            if d.eng == "pe" and op.eng == "pe":
                return True
            if d.eng == op.eng and d.eng in self.nosync_engines:
                return True
            return False

        for op in self.ops:
            for d in op.deps:
                if d.is_dma or skip(d, op):
                    continue
                d.needed = True
        ms_cnt = {e: 0 for e in ENGS}
        for op in self.ops:
            if not op.is_dma and op.needed:
                ms_cnt[op.eng] += 1
                op.ms = ms_cnt[op.eng]
        self.stats = dict(n_ops=len(self.ops), ms=dict(ms_cnt), dma=dict(dma_cnt))

        with contextlib.ExitStack() as st:
            esem = {e: st.enter_context(nc.semaphore("s_" + e)) for e in ENGS}
            dsem = {}
            for e in ENGS:
                for k in range(N_DMA_SEMS):
                    if dma_cnt[e] > k:
                        dsem[(e, k)] = st.enter_context(nc.semaphore("d_%s%d" % (e, k)))
            if n_cc:
                dsem[("cc", 0)] = st.enter_context(nc.semaphore("d_cc"))
            final_waits = {}
            for op in self.ops:
                if op.is_dma:
                    final_waits[op.dsem] = max(final_waits.get(op.dsem, 0), op.dcount)

            waited = {e: {} for e in ENGS}
            know = {}
            plan = {}

            def keyval(d):
                if d.is_dma:
                    return ("d",) + d.dsem, d.dcount
                return ("e", d.eng), d.ms
            for op in self.ops:
                w = waited[op.eng]
                needs, prod = {}, {}
                for d in op.deps:
                    if not d.is_dma and skip(d, op):
                        continue
                    k, v = keyval(d)
                    if needs.get(k, 0) < v:
                        needs[k] = v
                        prod[k] = d
                lst = []
                for k in sorted(needs, key=lambda kk: -prod[kk].idx):
                    v = needs[k]
                    if w.get(k, 0) >= v:
                        continue
                    lst.append((k, v))
                    w[k] = v
                    kd = know.get(prod[k])
                    if kd:
                        for kk, vv in kd.items():
                            if w.get(kk, 0) < vv:
                                w[kk] = vv
                plan[op] = lst
                if op.is_dma or op.ms is not None:
                    snap = dict(w)
                    k, v = keyval(op)
                    snap[k] = v
                    know[op] = snap
            self.stats["waits"] = sum(len(v) for v in plan.values())

            def run_stream(ename, h):
                for op in streams[ename]:
                    for (k, val) in plan[op]:
                        s_ = esem[k[1]] if k[0] == "e" else dsem[(k[1], k[2])]
                        h.wait_ge(s_, val)
                    inst = op.fn(h)
                    if op.is_dma:
                        inst.then_inc(dsem[op.dsem], 1 if op.cc else 16)
                    elif op.ms is not None:
                        inst.then_inc(esem[ename], 1)
                if ename == "sp":
                    for key, val in final_waits.items():
                        h.wait_ge(dsem[key], val)

            with nc.Block() as block:
                @block.sync
                def _(h):
                    run_stream("sp", h)

                @block.tensor
                def _(h):
                    run_stream("pe", h)

                @block.scalar
                def _(h):
                    run_stream("act", h)

                @block.vector
                def _(h):
                    run_stream("dve", h)

                @block.gpsimd
                def _(h):
                    run_stream("pool", h)


class Arena:
    def __init__(self, space, tensor_f32, nbytes):
        self.space, self.t, self.nbytes, self.off = space, tensor_f32, nbytes, 0

    def alloc(self, nelem, dtype, align=32):
        esz = 4 if dtype == F32 else 2
        self.off = (self.off + align - 1) // align * align
        a = self.off
        self.off += nelem * esz
        assert self.off <= self.nbytes, "arena overflow %s %d > %d" % (self.space, self.off, self.nbytes)
        return Buf(self, a, nelem, dtype)

    def at(self, byte_off, nelem, dtype):
        return Buf(self, byte_off, nelem, dtype)


class Buf:
    def __init__(self, arena, byte_off, nelem, dtype):
        self.arena, self.off, self.n, self.dtype = arena, byte_off, nelem, dtype
        self.esz = 4 if dtype == F32 else 2
        assert byte_off % 4 == 0
        w0 = byte_off // 4
        w1 = (byte_off + nelem * self.esz + 3) // 4
        ap = arena.t[:, w0:w1]
        if dtype != F32:
            ap = ap.bitcast(dtype)[:, 0:nelem]
        self.ap = ap

    def reg(self, e0=0, e1=None):
        if e1 is None:
            e1 = self.n
        return Reg(self.arena.space, self.off + e0 * self.esz, self.off + e1 * self.esz)

    def v3(self, b):
        return self.ap.rearrange("p (a b) -> p a b", b=b)

    def v4(self, b, c):
        return self.ap.rearrange("p (a b c) -> p a b c", b=b, c=c)


def bc(ap, axis, shape):
    return ap.unsqueeze(axis).to_broadcast(shape)


def build_nc(cfg):
    kinds = cfg["blocks"]
    groups = cfg["groups"]
    NBLK = len(kinds)
    NPRE = cfg.get("npre", 0)
    NBMAX = max(len(g) for g in groups)
    NMAX = NBMAX * 128
    last_main = max(i for i, k in enumerate(kinds) if k == "M")

    nc = bass.Bass("TRN2", target_bir_lowering=False)

    def din(name, shape, dt=F32):
        return nc.dram_tensor(name, list(shape), dt, kind="ExternalInput").ap()

    def dout(name, shape, dt=F32):
        return nc.dram_tensor(name, list(shape), dt, kind="ExternalOutput").ap()

    xs = din("xs", [NBLK * 128, D])
    tmask_d = din("tmask", [128, NBLK])
    gkeep_d = din("gkeep", [128, 1])
    c_conv_a = din("c_conv_a", [2, HA, D])
    c_ssm = din("c_ssm", [2, HX, 2048])
    c_state = din("c_state", [2, NH * HP, 128])
    c_ffn = din("c_ffn", [2, HG, DFF])
    w_in_d = din("w_in", [D, DIN])
    w_out_d = din("w_out", [2048, D])
    w_up_d = din("w_up", [D, 2 * DFF])
    w_down_d = din("w_down", [DFF, D])
    norm_mix_d = din("norm_mix", [128, KC])
    norm_ffn_d = din("norm_ffn", [128, KC])
    ssm_norm_d = din("ssm_norm", [128, KC])
    norm_final_d = din("norm_final", [1, D])
    wA_d = din("wA", [128, CA * WA])
    bA_d = din("bA", [128, CA])
    lng_d = din("lng", [128, CA])
    lnb_d = din("lnb", [128, CA])
    wS_d = din("wS", [128, CX * WX])
    bS_d = din("bS", [128, CX])
    wF_d = din("wF", [128, CF * WF])
    bF_d = din("bF", [128, CF])
    hvec_d = din("hvec", [1, 3 * NH])
    ident_d = din("ident", [128, 128])
    triu_d = din("triu", [128, 128])
    lmat_d = din("lmat", [128, 128])
    NCORES_ALL = cfg.get("ncores", 8)
    NCORES = cfg.get("nseg", 4)
    RGROUPS = [list(range(g * NCORES, (g + 1) * NCORES)) for g in range(NCORES_ALL // NCORES)]
    if NPRE:
        pmask_d = din("pmask", [128, NPRE])
        cmask_d = din("cmask", [128, NCORES])
        xpre = xs[(NBLK - NPRE) * 128:, :]
        cc_in = nc.dram_tensor("cc_in", [129, D], F32, kind="Internal").ap()
        cc_out = nc.dram_tensor("cc_out", [NCORES * 129, D], F32, kind="Internal").ap()

    y_d = dout("y", [NBLK * 128, D])
    o_conv_a_s = dout("o_conv_a_s", [2, HA, D])
    o_ssm_s = dout("o_ssm_s", [2, HX, 2048])
    o_state_s = dout("o_state_s", [2, NH * HP, 128])
    o_ffn_s = dout("o_ffn_s", [2, HG, DFF])
    o_conv_a_p = dout("o_conv_a_p", [HA, D])
    o_ssm_p = dout("o_ssm_p", [HX, 2048])
    o_state_p = dout("o_state_p", [NH * HP, 128])
    o_ffn_p = dout("o_ffn_p", [HG, DFF])

    wb_in = nc.dram_tensor("wb_in", [D, DIN], BF16, kind="Internal").ap()
    wb_out = nc.dram_tensor("wb_out", [2048, D], BF16, kind="Internal").ap()
    wb_up = nc.dram_tensor("wb_up", [D, 2 * DFF], BF16, kind="Internal").ap()
    wb_down = nc.dram_tensor("wb_down", [DFF, D], BF16, kind="Internal").ap()
    hpre_d = nc.dram_tensor("hpre_d", [128, D], F32, kind="Internal").ap()
    dg_dram = {"A": nc.dram_tensor("dgA", [CA, 128, WA * 128], BF16, kind="Internal").ap(),
               "S": nc.dram_tensor("dgS", [CX, 128, WX * 128], BF16, kind="Internal").ap(),
               "F": nc.dram_tensor("dgF", [CF, 128, WF * 128], BF16, kind="Internal").ap()}

    SB_BYTES = 207 * 1024
    with contextlib.ExitStack() as st:
        sbt = st.enter_context(nc.sbuf_tensor("arena", [128, SB_BYTES // 4], F32))
        pst = st.enter_context(nc.psum_tensor("psum", [128, 4096], F32))
        sb = Arena("sb", sbt, SB_BYTES)
        psA = Arena("ps", pst, 16384)
        S = Sched(nc)
        S.nosync_engines = cfg.get("nosync", ())

        ps_state = {"bank": 0}

        def PS(nelem, dtype=F32, nbanks=1):
            b = ps_state["bank"]
            if b + nbanks > 8:
                b = 0
            ps_state["bank"] = (b + nbanks) % 8
            return psA.at(b * 2048, nelem, dtype)

        identf = sb.alloc(128, F32)
        identb = sb.alloc(128, BF16)
        triu = sb.alloc(128, F32)
        lmat = sb.alloc(128, F32)
        onesf = sb.alloc(128, F32)
        onesb = sb.alloc(128, BF16)
        nfin = sb.alloc(D, F32)
        g_mix = sb.alloc(KC, F32)
        g_ffn = sb.alloc(KC, F32)
        g_ssm = sb.alloc(KC, F32)
        wA = sb.alloc(CA * WA, F32)
        bA = sb.alloc(CA, F32)
        lng = sb.alloc(CA, F32)
        lnb = sb.alloc(CA, F32)
        wS = sb.alloc(CX * WX, F32)
        bS = sb.alloc(CX, F32)
        wF = sb.alloc(CF * WF, F32)
        bF = sb.alloc(CF, F32)
        hvec = sb.alloc(3 * NH, F32)
        tmask = sb.alloc(NBLK, F32)
        gkeep = sb.alloc(1, F32)
        epsc = sb.alloc(1, F32)
        onec = sb.alloc(1, F32)
        if NPRE:
            pmask = sb.alloc(NPRE, F32)
            cmask = sb.alloc(NCORES, F32)
            atacc = sb.alloc(NH, F32)
        uh = sb.alloc(CA * HA, BF16)
        xh = sb.alloc(CX * HX, BF16)
        gh = sb.alloc(CF * HG, BF16)
        hT = sb.alloc(D, F32)
        hTb = sb.alloc(D, BF16)
        pxh = sb.alloc(12 * HX, BF16)
        NSLOT = 3
        wslots = [sb.alloc(KC * 512, BF16) for _ in range(NSLOT)]
        xg = sb.alloc(NBMAX * D, F32)
        diagA = [sb.alloc(WA * 128, BF16) for _ in range(2)]
        diagS = [sb.alloc(WX * 128, BF16) for _ in range(2)]
        diagF = [sb.alloc(WF * 128, BF16) for _ in range(2)]
        small = sb.alloc(512, F32)
        R1 = sb.alloc(KC * NMAX, BF16)
        hnT = R1
        outa = R1
        R2 = sb.alloc(CA * (NMAX + HA), BF16)
        u_ext = R2
        outb = None
        base_ffn = (sb.off + 31) // 32 * 32
        xbc_ext = sb.alloc(CX * (NMAX + HX), BF16)
        sz = sb.alloc(NBMAX * D, F32)
        CT = sb.alloc(NG * NMAX, BF16)
        BT = sb.alloc(NG * NMAX, BF16)
        Btok = sb.alloc(NBMAX * NG * 128, BF16)
        base_ln = (sb.off + 31) // 32 * 32
        xtok = sb.alloc(NBMAX * D, F32)
        rhs_seg = sb.alloc(NH * 128, F32)
        Emat = sb.alloc(NH * 128, F32)
        MT = sb.alloc(NH * 128, BF16)
        cbTm = sb.alloc(NG * 128, F32)
        xdt = sb.alloc(D, BF16)
        xdts = sb.alloc(D, BF16)
        t1 = sb.alloc(D, F32)
        t2 = sb.alloc(D, F32)
        t1b = sb.alloc(D, F32)
        ygn = sb.alloc(D, BF16)
        end_ssd = sb.off
        xnb = sb.at(cbTm.off, D, BF16)
        tmpf = [sb.alloc(NMAX, F32) for _ in range(2)]
        cacheTok = sb.at(t1.off, D, F32)
        tailb = sb.at(t2.off, D, F32)
        outb = sb.alloc(KC * NMAX, BF16)
        gate_ext = sb.at(base_ffn, CF * (NMAX + HG), BF16)
        actT = sb.at((gate_ext.off + gate_ext.n * 2 + 31) // 32 * 32, CF * NMAX, BF16)
        assert actT.off + actT.n * 2 <= end_ssd, "ffn region overflow"
        ucf = sb.at(xbc_ext.off, CA * NMAX, F32)
        assert ucf.n * 4 <= xbc_ext.n * 2
        ucb = sb.at(rhs_seg.off, CA * NMAX, BF16)
        sqb = sb.at(ucb.off + ucb.n * 2, CA * NMAX, BF16)
        assert sqb.off + sqb.n * 2 <= MT.off, "ln region overflow"
        yst = sb.at(sz.off, NBMAX * D, F32)
        mu = tmpf[0]
        rstdN = tmpf[1]

        def DMA(q, out_ap, in_ap, reads=(), writes=(), name=""):
            S.add(q, lambda h, o=out_ap, i=in_ap: h.dma_start(out=o, in_=i), reads=reads, writes=writes,
                  dma=True, name=name)

        def MM(out_ap, pairs, reads, writes, start=True, stop=True, name=""):
            def fn(h, out_ap=out_ap, pairs=pairs, start=start, stop=stop):
                n = len(pairs)
                inst = None
                for i, (l, r) in enumerate(pairs):
                    inst = h.matmul(out_ap, lhsT=l, rhs=r, start=(start and i == 0), stop=(stop and i == n - 1))
                return inst
            S.add("pe", fn, reads=reads, writes=writes, name=name)

        def MMS(items, reads, writes, name=""):
            def fn(h, items=items):
                inst = None
                for (o, l, r) in items:
                    inst = h.matmul(o, lhsT=l, rhs=r, start=True, stop=True)
                return inst
            S.add("pe", fn, reads=reads, writes=writes, name=name)

        def TR(items, reads, writes, name=""):
            def fn(h, items=items):
                inst = None
                for (o, i, idn) in items:
                    inst = h.transpose(out=o, in_=i, identity=idn)
                return inst
            S.add("pe", fn, reads=reads, writes=writes, name=name)

        def ACT(out_ap, in_ap, func, reads, writes, bias=None, scale=None, accum=None, name=""):
            kw = {}
            if bias is not None:
                kw["bias"] = bias
            if scale is not None:
                kw["scale"] = scale
            if accum is not None:
                kw["accum_out"] = accum
            S.add("act", lambda h, o=out_ap, i=in_ap, f=func, kw=kw: h.activation(out=o, in_=i, func=f, **kw),
                  reads=reads, writes=writes, name=name)

        def TT(eng, out_ap, in0, in1, op, reads, writes, name=""):
            S.add(eng, lambda h, o=out_ap, a=in0, b=in1, op=op: h.tensor_tensor(out=o, in0=a, in1=b, op=op),
                  reads=reads, writes=writes, name=name)

        def TS(eng, out_ap, in0, s1, op0, reads, writes, s2=None, op1=None, name=""):
            def fn(h, o=out_ap, a=in0, s1=s1, s2=s2, op0=op0, op1=op1):
                if op1 is None:
                    return h.tensor_scalar(out=o, in0=a, scalar1=s1, scalar2=None, op0=op0)
                return h.tensor_scalar(out=o, in0=a, scalar1=s1, scalar2=s2, op0=op0, op1=op1)
            S.add(eng, fn, reads=reads, writes=writes, name=name)

        def STT(eng, out_ap, in0, scalar, in1, op0, op1, reads, writes, name=""):
            S.add(eng, lambda h, o=out_ap, a=in0, s=scalar, b=in1, op0=op0, op1=op1:
                  h.scalar_tensor_tensor(out=o, in0=a, scalar=s, in1=b, op0=op0, op1=op1),
                  reads=reads, writes=writes, name=name)

        def CP(eng, out_ap, in_ap, reads, writes, name=""):
            if eng == "act":
                S.add("act", lambda h, o=out_ap, i=in_ap: h.copy(out=o, in_=i), reads=reads, writes=writes, name=name)
            else:
                S.add(eng, lambda h, o=out_ap, i=in_ap: h.tensor_copy(out=o, in_=i), reads=reads, writes=writes,
                      name=name)

        def MEMSET(eng, buf, val):
            S.add(eng, lambda h, b=buf, v=val: h.memset(b.ap, v), writes=[buf.reg()])

        def dreg(name, a=0, b=1 << 30):
            return Reg("dram:" + name, a, b)

        for (buf, src) in ((identf, ident_d), (triu, triu_d), (lmat, lmat_d), (g_mix, norm_mix_d),
                           (g_ffn, norm_ffn_d), (g_ssm, ssm_norm_d), (wA, wA_d), (bA, bA_d), (lng, lng_d),
                           (lnb, lnb_d), (wS, wS_d), (bS, bS_d), (wF, wF_d), (bF, bF_d), (tmask, tmask_d),
                           (gkeep, gkeep_d)):
            DMA("sp", buf.ap, src, writes=[buf.reg()])
        if NPRE:
            DMA("sp", pmask.ap, pmask_d, writes=[pmask.reg()])
            DMA("sp", cmask.ap, cmask_d, writes=[cmask.reg()])
            MEMSET("dve", atacc, 0.0)
        DMA("sp", nfin.ap, norm_final_d.partition_broadcast(128), writes=[nfin.reg()])
        DMA("sp", hvec.ap, hvec_d.partition_broadcast(128), writes=[hvec.reg()])
        MEMSET("dve", onesf, 1.0)
        MEMSET("dve", onesb, 1.0)
        MEMSET("dve", epsc, EPS)
        MEMSET("dve", onec, 1.0)
        MEMSET("dve", uh, 0.0)
        MEMSET("dve", xh, 0.0)
        MEMSET("dve", gh, 0.0)
        MEMSET("dve", hT, 0.0)
        MEMSET("dve", hTb, 0.0)
        CP("dve", identb.ap, identf.ap, [identf.reg()], [identb.reg()])
        ACT(hvec.ap[:, NH:2 * NH], hvec.ap[:, NH:2 * NH], AF.Exp, [hvec.reg()], [hvec.reg()])
        TS("dve", hvec.ap[:, NH:2 * NH], hvec.ap[:, NH:2 * NH], -1.0, ALU.mult, [hvec.reg()], [hvec.reg()])
        dtb_ap = hvec.ap[:, 0:NH]
        A_ap = hvec.ap[:, NH:2 * NH]
        Dsk_ap = hvec.ap[:, 2 * NH:3 * NH]

        def convert(src, dst, name, rows, rb=128):
            for r0 in range(0, rows, rb):
                r1 = min(rows, r0 + rb)
                DMA("pool", dst[r0:r1, :], src[r0:r1, :], writes=[dreg(name, r0, r1)])
        for r0 in range(0, D, 256):
            DMA("pool", wb_in[r0:r0 + 256, 3072:DIN], w_in_d[r0:r0 + 256, 3072:DIN], writes=[dreg("wb_inx", r0, r0 + 256)])
        conv_q = []
        for r0 in range(0, D, 128):
            conv_q.append((wb_in[r0:r0 + 128, 0:3072], w_in_d[r0:r0 + 128, 0:3072], dreg("wb_in", r0, r0 + 128)))
        for (src_, dst_, nm_, rows_) in ((w_out_d, wb_out, "wb_out", 2048), (w_up_d, wb_up, "wb_up", D),
                                         (w_down_d, wb_down, "wb_down", DFF)):
            for r0 in range(0, rows_, 128):
                r1 = min(rows_, r0 + 128)
                conv_q.append((dst_[r0:r1, :], src_[r0:r1, :], dreg(nm_, r0, r1)))

        def emit_conversions(n):
            for _ in range(n):
                if conv_q:
                    o_, i_, w_ = conv_q.pop(0)
                    DMA("pool", o_, i_, writes=[w_])

        wstate = {"i": 0}

        def load_piece(wb, name, r0, nk, c0, ncols):
            slot = wslots[wstate["i"] % NSLOT]
            wstate["i"] += 1
            src = wb[r0:r0 + nk * 128, c0:c0 + ncols].rearrange("(k p) n -> p k n", p=128)
            dstv = slot.ap[:, 0:nk * ncols].rearrange("p (k n) -> p k n", n=ncols)
            rd_ = [dreg(name, r0, r0 + nk * 128)]
            if name == "wb_in":
                rd_.append(dreg("wb_inx", r0, r0 + nk * 128))
            DMA("sp", dstv, src, reads=rd_, writes=[slot.reg(0, nk * ncols)])
            return dstv, slot.reg(0, nk * ncols)

        xnbs = [xnb, ygn]

        def norm_transpose_group(nb, gcol, dstT, ncolsT):
            for b in range(nb):
                ACT(MT.ap[:, 0:D], xg.ap[:, b * D:(b + 1) * D], AF.Square, [xg.reg(b * D, (b + 1) * D)],
                    [MT.reg(0, D), small.reg(b, b + 1)], scale=1.0 / 32.0, accum=small.ap[:, b:b + 1])
            ACT(small.ap[:, 4:4 + nb], small.ap[:, 0:nb], AF.Ln, [small.reg(0, nb), epsc.reg()], [small.reg(4, 4 + nb)],
                bias=epsc.ap)
            ACT(small.ap[:, 8:8 + nb], small.ap[:, 4:4 + nb], AF.Exp, [small.reg(4, 4 + nb)], [small.reg(8, 8 + nb)],
                scale=-0.5)
            for b in range(nb):
                xb = xnbs[b % 2]
                TS("dve", xb.ap, xg.ap[:, b * D:(b + 1) * D], small.ap[:, 8 + b:9 + b], ALU.mult,
                   [xg.reg(b * D, (b + 1) * D), small.reg(8 + b, 9 + b)], [xb.reg()])
                pT = PS(KC * 128, BF16)
                TR([(pT.ap[:, k * 128:(k + 1) * 128], xb.ap[:, k * 128:(k + 1) * 128], identb.ap) for k in range(KC)],
                   [xb.reg(), identb.reg()], [pT.reg()])
                dv = dstT.ap.rearrange("p (k n) -> p k n", n=ncolsT)[:, :, b * 128:(b + 1) * 128]
                wregs = [dstT.reg(k * ncolsT + b * 128, k * ncolsT + (b + 1) * 128) for k in range(KC)]
                TT("dve", dv, pT.v3(128), bc(gcol.ap, 2, [128, KC, 128]), ALU.mult, [pT.reg(), gcol.reg()], wregs)

        def norm_chain_block(b):
            o = 24 + 3 * b
            xb = xnbs[b % 2]
            ACT(MT.ap[:, 0:D], xg.ap[:, b * D:(b + 1) * D], AF.Square, [xg.reg(b * D, (b + 1) * D)],
                [MT.reg(0, D), small.reg(o, o + 1)], scale=1.0 / 32.0, accum=small.ap[:, o:o + 1])
            ACT(small.ap[:, o + 1:o + 2], small.ap[:, o:o + 1], AF.Ln, [small.reg(o, o + 1), epsc.reg()],
                [small.reg(o + 1, o + 2)], bias=epsc.ap)
            ACT(small.ap[:, o + 2:o + 3], small.ap[:, o + 1:o + 2], AF.Exp, [small.reg(o + 1, o + 2)],
                [small.reg(o + 2, o + 3)], scale=-0.5)
            TS("dve", xb.ap, xg.ap[:, b * D:(b + 1) * D], small.ap[:, o + 2:o + 3], ALU.mult,
               [xg.reg(b * D, (b + 1) * D), small.reg(o + 2, o + 3)], [xb.reg()])

        def norm_tr_block(b, gcol, dstT, ncolsT):
            xb = xnbs[b % 2]
            pT = PS(KC * 128, BF16)
            TR([(pT.ap[:, k * 128:(k + 1) * 128], xb.ap[:, k * 128:(k + 1) * 128], identb.ap) for k in range(KC)],
               [xb.reg(), identb.reg()], [pT.reg()])
            dv = dstT.ap.rearrange("p (k n) -> p k n", n=ncolsT)[:, :, b * 128:(b + 1) * 128]
            wregs = [dstT.reg(k * ncolsT + b * 128, k * ncolsT + (b + 1) * 128) for k in range(KC)]
            TT("dve", dv, pT.v3(128), bc(gcol.ap, 2, [128, KC, 128]), ALU.mult, [pT.reg(), gcol.reg()], wregs)

        def norm_transpose_block(b, gcol, dstT, ncolsT):
            norm_chain_block(b)
            norm_tr_block(b, gcol, dstT, ncolsT)

        def state_load(src_dram):
            DMA("sp", cacheTok.ap.rearrange("p (j n) -> p j n", n=128),
                src_dram.rearrange("(j q) n -> q j n", q=128), writes=[cacheTok.reg()])
            pS = psA.at(5 * 2048, D, F32)
            TR([(pS.ap[:, j * 128:(j + 1) * 128], cacheTok.ap[:, j * 128:(j + 1) * 128], identf.ap) for j in range(8)],
               [cacheTok.reg(), identf.reg()], [pS.reg()])
            CP("dve", hT.ap, pS.ap, [pS.reg()], [hT.reg()])
            CP("act", hTb.ap, hT.ap, [hT.reg()], [hTb.reg()])

        def state_export(dst_dram):
            pS = psA.at(5 * 2048, D, F32)
            TR([(pS.ap[:, j * 128:(j + 1) * 128], hT.ap[:, j * 128:(j + 1) * 128], identf.ap) for j in range(8)],
               [hT.reg(), identf.reg()], [pS.reg()])
            CP("dve", tailb.ap, pS.ap, [pS.reg()], [tailb.reg()])
            DMA("pool", dst_dram.rearrange("(j q) n -> q j n", q=128),
                tailb.ap.rearrange("p (j n) -> p j n", n=128), reads=[tailb.reg()])

        def ssd_small(ps_dt, mask_ap, mask_reg, nb):
            o_dt = 16
            o_a = 80
            W = nb * NH
            dt_b = small.ap[:, o_dt:o_dt + W]
            a_b = small.ap[:, o_a:o_a + W]
            r_dt = small.reg(o_dt, o_dt + W)
            r_a = small.reg(o_a, o_a + W)
            TT("dve", dt_b.rearrange("p (b h) -> p b h", h=NH), ps_dt.v3(NH), bc(dtb_ap, 1, [128, nb, NH]), ALU.add,
               [ps_dt.reg(), hvec.reg()], [r_dt])
            ACT(dt_b, dt_b, AF.Exp, [r_dt], [r_dt])
            ACT(dt_b, dt_b, AF.Ln, [r_dt, onec.reg()], [r_dt], bias=onec.ap)
            TT("dve", dt_b.rearrange("p (b h) -> p b h", h=NH), dt_b.rearrange("p (b h) -> p b h", h=NH),
               bc(mask_ap, 2, [128, nb, NH]), ALU.mult, [r_dt, mask_reg], [r_dt])
            TT("dve", a_b.rearrange("p (b h) -> p b h", h=NH), dt_b.rearrange("p (b h) -> p b h", h=NH),
               bc(A_ap, 1, [128, nb, NH]), ALU.mult, [r_dt, hvec.reg()], [r_a])
            return o_dt, o_a

        def ssd_decays_all(o_dt, o_a, nb, group_level=False):
            W = nb * NH
            a_all = small.ap[:, o_a:o_a + W]
            r_a = small.reg(o_a, o_a + W)
            pc = PS(2 * W, F32)
            if not group_level:
                MMS([(pc.ap[:, 0:W], triu.ap, a_all), (pc.ap[:, W:2 * W], onesf.ap, a_all)],
                    [triu.reg(), onesf.reg(), r_a], [pc.reg()])
            else:
                MMS([(pc.ap[:, 0:W], triu.ap, a_all)], [triu.reg(), r_a], [pc.reg()])
                for b in range(nb):
                    MM(pc.ap[:, W + b * NH:W + (b + 1) * NH],
                       [(onesf.ap, small.ap[:, o_a + b2 * NH:o_a + (b2 + 1) * NH]) for b2 in range(b, nb)],
                       [onesf.reg(), r_a], [pc.reg()])
            o_ex = 160
            o_w2 = 352
            r_ex = small.reg(o_ex, o_ex + 3 * W)
            CP("act", small.ap[:, o_ex:o_ex + 2 * W], pc.ap, [pc.reg()], [small.reg(o_ex, o_ex + 2 * W)])
            TT("dve", small.ap[:, o_ex + 2 * W:o_ex + 3 * W], small.ap[:, o_ex + W:o_ex + 2 * W], small.ap[:, o_ex:o_ex + W],
               ALU.subtract, [small.reg(o_ex, o_ex + 2 * W)], [small.reg(o_ex + 2 * W, o_ex + 3 * W)])
            if group_level:
                TT("dve", atacc.ap, atacc.ap, small.ap[:, o_ex + W:o_ex + W + NH], ALU.add,
                   [atacc.reg(), small.reg(o_ex + W, o_ex + W + NH)], [atacc.reg()])
            ACT(small.ap[:, o_ex:o_ex + 3 * W], small.ap[:, o_ex:o_ex + 3 * W], AF.Exp, [r_ex], [r_ex])
            TT("dve", small.ap[:, o_w2:o_w2 + W], small.ap[:, o_dt:o_dt + W], small.ap[:, o_ex + 2 * W:o_ex + 3 * W],
               ALU.mult, [small.reg(o_dt, o_dt + W), r_ex], [small.reg(o_w2, o_w2 + W)])

            def sl(o, b):
                return (small.ap[:, o + b * NH:o + (b + 1) * NH], small.reg(o + b * NH, o + (b + 1) * NH))
            return [dict(a=sl(o_a, b), dt=sl(o_dt, b), ea=sl(o_ex, b), eatot=sl(o_ex + W, b), w2=sl(o_w2, b))
                    for b in range(nb)]

        def state_update(dec, b, Btok_blk_ap, Btok_reg, xtok_ap, xtok_reg):
            w2_ap, w2_reg = dec["w2"]
            TT("pool", xdts.ap.rearrange("p (h q) -> p h q", q=HP), xtok_ap.rearrange("p (h q) -> p h q", q=HP),
               bc(w2_ap, 2, [128, NH, HP]), ALU.mult, [xtok_reg, w2_reg], [xdts.reg()])
            pSt = PS(D, F32, nbanks=2)
            MMS([(pSt.ap[:, g * 256:(g + 1) * 256], Btok_blk_ap[:, g * 128:(g + 1) * 128],
                  xdts.ap[:, g * 256:(g + 1) * 256]) for g in range(NG)],
                [Btok_reg, xdts.reg()], [pSt.reg()])
            et_ap, et_reg = dec["eatot"]
            TT("pool", hT.ap.rearrange("p (h q) -> p h q", q=HP), hT.ap.rearrange("p (h q) -> p h q", q=HP),
               bc(et_ap, 2, [128, NH, HP]), ALU.mult, [hT.reg(), et_reg], [hT.reg()])
            TT("dve", hT.ap, hT.ap, pSt.ap, ALU.add, [hT.reg(), pSt.reg()], [hT.reg()])
            CP("act", hTb.ap, hT.ap, [hT.reg()], [hTb.reg()])

        dstate = {"A": 0, "S": 0, "F": 0}

        DG = {"A": (diagA, wA, WA, CA), "S": (diagS, wS, WX, CX), "F": (diagF, wF, WF, CF)}
        pending_builds = [(k, c) for k in ("A", "S", "F") for c in range(DG[k][3])]

        def emit_builds(n):
            for _ in range(n):
                if not pending_builds:
                    return
                kind, c = pending_builds.pop(0)
                bufs, wbuf, W, _n = DG[kind]
                dbuf = bufs[dstate[kind] % 2]
                dstate[kind] += 1
                TT("pool", dbuf.v3(128), bc(identb.ap, 1, [128, W, 128]),
                   bc(wbuf.ap[:, c * W:(c + 1) * W], 2, [128, W, 128]), ALU.mult,
                   [identb.reg(), wbuf.reg()], [dbuf.reg()])
                DMA("pool", dg_dram[kind][c], dbuf.ap, reads=[dbuf.reg()], writes=[dreg("dg" + kind, c, c + 1)])

        def build_diag(kind, c):
            bufs, wbuf, W, _n = DG[kind]
            dbuf = bufs[dstate[kind] % 2]
            dstate[kind] += 1
            DMA("sp", dbuf.ap, dg_dram[kind][c], reads=[dreg("dg" + kind, c, c + 1)], writes=[dbuf.reg()])
            return dbuf

        if NPRE:
            PG = 4
            wpx = sb.at(xbc_ext.off, KC * 1536, BF16)
            wpd = sb.at(t2.off, KC * NH, BF16)
            DMA("sp", wpx.v3(1536), wb_in[:, 3072:4608].rearrange("(k p) n -> p k n", p=128),
                reads=[dreg("wb_inx")], writes=[wpx.reg()])
            DMA("sp", wpd.v3(NH), wb_in[:, 5120:5136].rearrange("(k p) n -> p k n", p=128),
                reads=[dreg("wb_inx")], writes=[wpd.reg()])
            pxe = sb.at(rhs_seg.off, 12 * (512 + HX), BF16)
            assert pxe.off + pxe.n * 2 <= MT.off
            MEMSET("dve", pxh, 0.0)
            pxb = [sb.at(tmpf[0].off, NMAX, BF16), sb.at(tmpf[0].off + NMAX * 2, NMAX, BF16)]
            pdiag = sb.at((wpx.off + wpx.n * 2 + 31) // 32 * 32, 12 * WX * 128, BF16)
            assert pdiag.off + pdiag.n * 2 <= BT.off, "pdiag overflow"
            pdv = pdiag.ap.rearrange("p (c k j) -> p c k j", k=WX, j=128)
            for c in range(12):
                TT("pool", pdv[:, c], bc(identb.ap, 1, [128, WX, 128]),
                   bc(wS.ap[:, c * WX:(c + 1) * WX], 2, [128, WX, 128]), ALU.mult,
                   [identb.reg(), wS.reg()], [pdiag.reg(c * WX * 128, (c + 1) * WX * 128)])
            xdts_all = sb.at(xdt.off, PG * D, BF16)
            assert xdts_all.off + xdts_all.n * 2 <= t2.off
            ngroups_pre = (NPRE + PG - 1) // PG
            builds_per_group = (len(pending_builds) + ngroups_pre - 1) // ngroups_pre
            for g0 in range(0, NPRE, PG):
                nb = min(PG, NPRE - g0)
                N = nb * 128
                DMA("sp", xg.ap[:, 0:nb * D].rearrange("p (b d) -> p b d", d=D),
                    xpre[g0 * 128:(g0 + nb) * 128, :].rearrange("(b p) d -> p b d", p=128),
                    writes=[xg.reg(0, nb * D)])
                norm_transpose_group(nb, g_mix, hnT, NMAX)
                hv = hnT.v3(NMAX)
                pv = pxe.v3(512 + HX)
                CP("pool", pv[:, :, 0:HX], pxh.v3(HX), [pxh.reg()], [pxe.reg()])
                for c in range(12):
                    pp = PS(N, F32)
                    MM(pp.ap, [(wpx.v3(1536)[:, k, c * 128:(c + 1) * 128], hv[:, k, 0:N]) for k in range(KC)],
                       [wpx.reg(), hnT.reg()], [pp.reg()])
                    CP("act" if c % 2 == 0 else "dve", pv[:, c, HX:HX + N], pp.ap, [pp.reg()],
                       [pxe.reg(c * (512 + HX) + HX, c * (512 + HX) + HX + N)])
                CP("pool", pxh.v3(HX), pv[:, :, N:N + HX], [pxe.reg()], [pxh.reg()])
                pdt = PS(nb * NH, F32)
                for b in range(nb):
                    MM(pdt.ap[:, b * NH:(b + 1) * NH],
                       [(hv[:, k, b * 128:(b + 1) * 128], wpd.v3(NH)[:, k, :]) for k in range(KC)],
                       [hnT.reg(), wpd.reg()], [pdt.reg(b * NH, (b + 1) * NH)])
                o_dt, o_a = ssd_small(pdt, pmask.ap[:, g0:g0 + nb], pmask.reg(), nb)
                decs = ssd_decays_all(o_dt, o_a, nb, group_level=True)
                def pre_conv(c):
                    pp = PS(N, F32)
                    MM(pp.ap, [(pdv[:, c, k, :], pv[:, c, k:k + N]) for k in range(WX)],
                       [pdiag.reg(c * WX * 128, (c + 1) * WX * 128), pxe.reg(c * (512 + HX), (c + 1) * (512 + HX))],
                       [pp.reg()])
                    if c < 8:
                        tb_ = pxb[c % 2]
                        ACT(tb_.ap[:, 0:N], pp.ap, AF.Silu, [pp.reg(), bS.reg()], [tb_.reg(0, N)], bias=bS.ap[:, c:c + 1])
                    else:
                        g = c - 8
                        ACT(BT.ap[:, g * NMAX:g * NMAX + N], pp.ap, AF.Silu, [pp.reg(), bS.reg()],
                            [BT.reg(g * NMAX, g * NMAX + N)], bias=bS.ap[:, c:c + 1])

                def pre_tr(c):
                    if c < 8:
                        tb_ = pxb[c % 2]
                        pt = PS(nb * 128, BF16)
                        TR([(pt.ap[:, b * 128:(b + 1) * 128], tb_.ap[:, b * 128:(b + 1) * 128], identb.ap)
                            for b in range(nb)], [tb_.reg(0, N), identb.reg()], [pt.reg()])
                        CP("dve", xtok.ap.rearrange("p (b d) -> p b d", d=D)[:, 0:nb, c * 128:(c + 1) * 128],
                           pt.v3(128), [pt.reg()],
                           [xtok.reg(b * D + c * 128, b * D + (c + 1) * 128) for b in range(nb)])
                    else:
                        g = c - 8
                        pt = PS(nb * 128, BF16)
                        TR([(pt.ap[:, b * 128:(b + 1) * 128], BT.ap[:, g * NMAX + b * 128:g * NMAX + (b + 1) * 128],
                             identb.ap) for b in range(nb)], [BT.reg(g * NMAX, g * NMAX + N), identb.reg()], [pt.reg()])
                        CP("dve", Btok.ap.rearrange("p (b n) -> p b n", n=NG * 128)[:, 0:nb, g * 128:(g + 1) * 128],
                           pt.v3(128), [pt.reg()],
                           [Btok.reg(b * NG * 128 + g * 128, b * NG * 128 + (g + 1) * 128) for b in range(nb)])
                pre_conv(0)
                for c in range(12):
                    if c + 1 < 12:
                        pre_conv(c + 1)
                    pre_tr(c)
                W = nb * NH
                w2_all = small.ap[:, 352:352 + W]
                TT("dve", xdts_all.ap[:, 0:nb * D].rearrange("p (b h q) -> p b h q", h=NH, q=HP),
                   xtok.ap[:, 0:nb * D].rearrange("p (b h q) -> p b h q", h=NH, q=HP),
                   bc(w2_all.rearrange("p (b h) -> p b h", h=NH), 3, [128, nb, NH, HP]), ALU.mult,
                   [xtok.reg(0, nb * D), small.reg(352, 352 + W)], [xdts_all.reg(0, nb * D)])
                pSt = PS(D, F32, nbanks=2)
                for g in range(NG):
                    MM(pSt.ap[:, g * 256:(g + 1) * 256],
                       [(Btok.ap[:, b * NG * 128 + g * 128:b * NG * 128 + (g + 1) * 128],
                         xdts_all.ap[:, b * D + g * 256:b * D + (g + 1) * 256]) for b in range(nb)],
                       [Btok.reg(), xdts_all.reg(0, nb * D)], [pSt.reg()])
                eg_ap, eg_reg = decs[0]["eatot"]
                TT("pool", hT.ap.rearrange("p (h q) -> p h q", q=HP), hT.ap.rearrange("p (h q) -> p h q", q=HP),
                   bc(eg_ap, 2, [128, NH, HP]), ALU.mult, [hT.reg(), eg_reg], [hT.reg()])
                TT("dve", hT.ap, hT.ap, pSt.ap, ALU.add, [hT.reg(), pSt.reg()], [hT.reg()])
                emit_builds(builds_per_group)
                emit_conversions(9)

            DMA("pool", cc_in[0:128, :], hT.ap, reads=[hT.reg()], writes=[dreg("cc_in")])
            S.add("dve", lambda h: h.memset(xtok.ap[0:1, 0:D], 0.0), writes=[xtok.reg(0, D)])
            CP("dve", xtok.ap[0:1, 0:NH], atacc.ap[0:1, :], [atacc.reg()], [xtok.reg(0, D)])
            DMA("pool", cc_in[128:129, :], xtok.ap[0:1, 0:D], reads=[xtok.reg(0, D)], writes=[dreg("cc_in")])
            S.add("pool", lambda h: h.collective_compute("AllGather", ALU.bypass, replica_groups=RGROUPS,
                                                         ins=[cc_in], outs=[cc_out]),
                  reads=[dreg("cc_in")], writes=[dreg("cc_out")], cc=True)
            MEMSET("dve", hT, 0.0)
            for r in range(NCORES):
                DMA("sp", small.ap[:, 440 + r * NH:440 + (r + 1) * NH],
                    cc_out[r * 129 + 128:r * 129 + 129, 0:NH].partition_broadcast(128),
                    reads=[dreg("cc_out")], writes=[small.reg(440 + r * NH, 440 + (r + 1) * NH)])
                DMA("sp", xtok.ap[:, r * D:(r + 1) * D], cc_out[r * 129:r * 129 + 128, :], reads=[dreg("cc_out")],
                    writes=[xtok.reg(r * D, (r + 1) * D)])
            for r in range(NCORES):
                arow = small.ap[:, 440 + r * NH:440 + (r + 1) * NH]
                r_arow = small.reg(440 + r * NH, 440 + (r + 1) * NH)
                TS("dve", arow, arow, cmask.ap[:, r:r + 1], ALU.mult, [r_arow, cmask.reg()], [r_arow])
                ACT(arow, arow, AF.Exp, [r_arow], [r_arow])
                TT("pool", hT.ap.rearrange("p (h q) -> p h q", q=HP), hT.ap.rearrange("p (h q) -> p h q", q=HP),
                   bc(arow, 2, [128, NH, HP]), ALU.mult, [hT.reg(), r_arow], [hT.reg()])
                STT("dve", hT.ap, xtok.ap[:, r * D:(r + 1) * D], cmask.ap[:, r:r + 1], hT.ap, ALU.mult, ALU.add,
                    [xtok.reg(r * D, (r + 1) * D), cmask.reg(), hT.reg()], [hT.reg()])
            DMA("pool", hpre_d, hT.ap, reads=[hT.reg()], writes=[dreg("hpre")])

        emit_conversions(len(conv_q))
        emit_builds(len(pending_builds))

        samp_of = {}
        si = 0
        for i, k in enumerate(kinds):
            if k in ("A", "B"):
                samp_of[i] = si
                si += 1

        def tail_site(blk):
            if kinds[blk] in ("A", "B"):
                return (HA, 16)
            if blk == last_main:
                return (128 - HA, HA)
            return None

        for gi, grp in enumerate(groups):
            nb = len(grp)
            N = nb * 128
            t0 = grp[0]
            assert grp == list(range(t0, t0 + nb))
            xgv = xg.ap[:, 0:nb * D].rearrange("p (b d) -> p b d", d=D)
            for b in range(nb):
                DMA("sp", xg.ap[:, b * D:(b + 1) * D], xs[(t0 + b) * 128:(t0 + b + 1) * 128, :],
                    writes=[xg.reg(b * D, (b + 1) * D)])
            for b in range(nb):
                norm_transpose_block(b, g_mix, hnT, NMAX)
            hv = hnT.v3(NMAX)
            uv = u_ext.v3(NMAX + HA)
            xv = xbc_ext.v3(NMAX + HX)
            UW = NMAX + HA
            XW = NMAX + HX
            CP("pool", uv[:, :, 0:HA], uh.v3(HA), [uh.reg()], [u_ext.reg()])
            CP("pool", xv[:, :, 0:HX], xh.v3(HX), [xh.reg()], [xbc_ext.reg()])

            tails = [(b, tail_site(t0 + b)) for b in range(nb) if tail_site(t0 + b) is not None]

            ag_state = {}

            def ag_a(c):
                half, j = c // 4, c % 4
                if j == 0:
                    ag_state["wa"] = load_piece(wb_in, "wb_in", 0, KC, half * 512, 512)
                    ag_state["wg"] = load_piece(wb_in, "wb_in", 0, KC, 1024 + half * 512, 512)
                wa, ra = ag_state["wa"]
                pa_ = psA.at(2 * 2048, N, F32)
                MM(pa_.ap, [(wa[:, k, j * 128:(j + 1) * 128], hv[:, k, 0:N]) for k in range(KC)],
                   [ra, hnT.reg()], [pa_.reg()])

            def ag_g(c):
                half, j = c // 4, c % 4
                wa, ra = ag_state["wa"]
                wg, rg = ag_state["wg"]
                pa_ = psA.at(2 * 2048, N, F32)
                pg_ = psA.at(7 * 2048, N, F32)
                MM(pg_.ap, [(wg[:, k, j * 128:(j + 1) * 128], hv[:, k, 0:N]) for k in range(KC)],
                   [rg, hnT.reg()], [pg_.reg()])
                tf = tmpf[c % 2]
                ACT(tf.ap[:, 0:N], pg_.ap, AF.Sigmoid, [pg_.reg()], [tf.reg(0, N)])
                TT("dve", uv[:, c, HA:HA + N], pa_.ap, tf.ap[:, 0:N], ALU.mult, [pa_.reg(), tf.reg(0, N)],
                   [u_ext.reg(c * UW + HA, c * UW + HA + N)])
                if j == 3:
                    for (b, (r0, nr)) in tails:
                        pa2 = psA.at(2 * 2048, 512, F32)
                        pg2 = psA.at(7 * 2048, 512, F32)
                        c0 = b * 128 + r0
                        MM(pa2.ap[0:nr, :], [(hv[:, k, c0:c0 + nr], wa[:, k, :]) for k in range(KC)],
                           [ra, hnT.reg()], [pa2.reg()])
                        MM(pg2.ap[0:nr, :], [(hv[:, k, c0:c0 + nr], wg[:, k, :]) for k in range(KC)],
                           [rg, hnT.reg()], [pg2.reg()])
                        tf = tmpf[0]
                        ACT(tf.ap[0:nr, 0:512], pg2.ap[0:nr, :], AF.Sigmoid, [pg2.reg()], [tf.reg(0, 512)])
                        TT("dve", tailb.ap[0:nr, 0:512], pa2.ap[0:nr, :], tf.ap[0:nr, 0:512], ALU.mult,
                           [pa2.reg(), tf.reg(0, 512)], [tailb.reg(0, 512)])
                        blk = t0 + b
                        cs = slice(half * 512, (half + 1) * 512)
                        if kinds[blk] in ("A", "B"):
                            s_ = samp_of[blk]
                            DMA("pool", o_conv_a_s[s_, HA - 16:HA, cs], tailb.ap[0:16, 0:512], reads=[tailb.reg(0, 512)])
                            if half == 0:
                                DMA("pool", o_conv_a_s[s_, 0:HA - 16, :], c_conv_a[s_, 16:HA, :])
                        else:
                            DMA("pool", o_conv_a_p[:, cs], tailb.ap[0:HA, 0:512], reads=[tailb.reg(0, 512)])
            fill_q = []
            for c_ in range(CA):
                fill_q.append((ag_a, c_))
                fill_q.append((ag_g, c_))

            def fill(n=1):
                for _ in range(n):
                    if fill_q:
                        f_, c_ = fill_q.pop(0)
                        f_(c_)
            for half in range(2):
                wz, rz = load_piece(wb_in, "wb_in", 0, KC, 2048 + half * 512, 512)
                for b in range(nb):
                    pz = PS(512, F32)
                    MM(pz.ap, [(hv[:, k, b * 128:(b + 1) * 128], wz[:, k, :]) for k in range(KC)],
                       [rz, hnT.reg()], [pz.reg()])
                    ACT(sz.ap[:, b * D + half * 512:b * D + (half + 1) * 512], pz.ap, AF.Silu, [pz.reg()],
                        [sz.reg(b * D + half * 512, b * D + (half + 1) * 512)])
            for q in range(4):
                wx, rx = load_piece(wb_in, "wb_in", 0, KC, 3072 + q * 512, 512)
                for j in range(4):
                    c = q * 4 + j
                    pp = PS(N, F32)
                    MM(pp.ap, [(wx[:, k, j * 128:(j + 1) * 128], hv[:, k, 0:N]) for k in range(KC)],
                       [rx, hnT.reg()], [pp.reg()])
                    CP("act" if c % 2 == 0 else "dve", xv[:, c, HX:HX + N], pp.ap, [pp.reg()],
                       [xbc_ext.reg(c * XW + HX, c * XW + HX + N)])
                for (b, (r0, nr)) in tails:
                    c0 = b * 128 + r0 + nr - HX
                    pp = PS(512, F32)
                    MM(pp.ap[0:HX, :], [(hv[:, k, c0:c0 + HX], wx[:, k, :]) for k in range(KC)],
                       [rx, hnT.reg()], [pp.reg()])
                    CP("dve", cacheTok.ap[0:HX, 0:512], pp.ap[0:HX, :], [pp.reg()], [cacheTok.reg()])
                    blk = t0 + b
                    dst = o_ssm_s[samp_of[blk]] if kinds[blk] in ("A", "B") else o_ssm_p
                    DMA("pool", dst[:, q * 512:(q + 1) * 512], cacheTok.ap[0:HX, 0:512], reads=[cacheTok.reg()])
            wd, rd = load_piece(wb_in, "wb_in", 0, KC, 5120, NH)
            pdt = PS(nb * NH, F32)
            for b in range(nb):
                MM(pdt.ap[:, b * NH:(b + 1) * NH], [(hv[:, k, b * 128:(b + 1) * 128], wd[:, k, :]) for k in range(KC)],
                   [hnT.reg(), rd], [pdt.reg(b * NH, (b + 1) * NH)])
            o_dt, o_a = ssd_small(pdt, tmask.ap[:, t0:t0 + nb], tmask.reg(), nb)
            for b in range(nb):
                blk = t0 + b
                if kinds[blk] in ("A", "B"):
                    s = samp_of[blk]
                    for hh in range(2):
                        DMA("sp", cacheTok.ap[0:HX, :], c_ssm[s, :, hh * D:(hh + 1) * D], writes=[cacheTok.reg()])
                        pi = PS(8 * HX, F32)
                        TR([(pi.ap[:, c * HX:(c + 1) * HX], cacheTok.ap[0:HX, c * 128:(c + 1) * 128],
                             identf.ap[0:HX, 0:HX]) for c in range(8)], [cacheTok.reg(), identf.reg()], [pi.reg()])
                        CP("dve", xv[:, hh * 8:(hh + 1) * 8, HX + b * 128 + HA - HX:HX + b * 128 + HA], pi.v3(HX),
                           [pi.reg()], [xbc_ext.reg()])
            CP("pool", xh.v3(HX), xv[:, :, N:N + HX], [xbc_ext.reg()], [xh.reg()])

            def sconv(c):
                dg = build_diag("S", c)
                pp = PS(N, F32)
                MM(pp.ap, [(dg.v3(128)[:, k, :], xv[:, c, k:k + N]) for k in range(WX)],
                   [dg.reg(), xbc_ext.reg(c * XW, (c + 1) * XW)], [pp.reg()])
                if c < 8:
                    tf = tmpf[c % 2]
                    ACT(tf.ap[:, 0:N], pp.ap, AF.Silu, [pp.reg(), bS.reg()], [tf.reg(0, N)], bias=bS.ap[:, c:c + 1])
                elif c < 12:
                    g = c - 8
                    ACT(BT.ap[:, g * NMAX:g * NMAX + N], pp.ap, AF.Silu, [pp.reg(), bS.reg()],
                        [BT.reg(g * NMAX, g * NMAX + N)], bias=bS.ap[:, c:c + 1])
                else:
                    g = c - 12
                    ACT(CT.ap[:, g * NMAX:g * NMAX + N], pp.ap, AF.Silu, [pp.reg(), bS.reg()],
                        [CT.reg(g * NMAX, g * NMAX + N)], bias=bS.ap[:, c:c + 1])

            def strans(c):
                if c < 8:
                    tf = tmpf[c % 2]
                    pt = PS(nb * 128, F32)
                    TR([(pt.ap[:, b * 128:(b + 1) * 128], tf.ap[:, b * 128:(b + 1) * 128], identf.ap)
                        for b in range(nb)], [tf.reg(0, N), identf.reg()], [pt.reg()])
                    CP("act", xtok.ap.rearrange("p (b d) -> p b d", d=D)[:, 0:nb, c * 128:(c + 1) * 128],
                       pt.v3(128), [pt.reg()],
                       [xtok.reg(b * D + c * 128, b * D + (c + 1) * 128) for b in range(nb)])
                elif c < 12:
                    g = c - 8
                    pt = PS(nb * 128, BF16)
                    TR([(pt.ap[:, b * 128:(b + 1) * 128], BT.ap[:, g * NMAX + b * 128:g * NMAX + (b + 1) * 128],
                         identb.ap) for b in range(nb)], [BT.reg(g * NMAX, g * NMAX + N), identb.reg()], [pt.reg()])
                    CP("act", Btok.ap.rearrange("p (b n) -> p b n", n=NG * 128)[:, 0:nb, g * 128:(g + 1) * 128],
                       pt.v3(128), [pt.reg()],
                       [Btok.reg(b * NG * 128 + g * 128, b * NG * 128 + (g + 1) * 128) for b in range(nb)])
            sconv(0)
            for c in range(CX):
                if c + 1 < CX:
                    sconv(c + 1)
                strans(c)
            decs = ssd_decays_all(o_dt, o_a, nb)

            obv = outb.v3(NMAX)

            def PSF(bank, nelem, dtype=F32):
                return psA.at(bank * 2048, nelem, dtype)
            t1s = [t1, t1b]

            def ssd_front(b):
                dec = decs[b]
                a_ap, a_reg = dec["a"]
                dt_ap, dt_reg = dec["dt"]
                xt_ap = xtok.ap[:, b * D:(b + 1) * D]
                xt_reg = xtok.reg(b * D, (b + 1) * D)
                TT("dve", xdt.ap.rearrange("p (h q) -> p h q", q=HP), xt_ap.rearrange("p (h q) -> p h q", q=HP),
                   bc(dt_ap, 2, [128, NH, HP]), ALU.mult, [xt_reg, dt_reg], [xdt.reg()])
                TT("pool", rhs_seg.v3(128), bc(triu.ap, 1, [128, NH, 128]), bc(a_ap, 2, [128, NH, 128]), ALU.mult,
                   [triu.reg(), a_reg], [rhs_seg.reg()])
                pcb = PSF(0, NG * 128)
                MMS([(pcb.ap[:, g * 128:(g + 1) * 128], BT.ap[:, g * NMAX + b * 128:g * NMAX + (b + 1) * 128],
                      CT.ap[:, g * NMAX + b * 128:g * NMAX + (b + 1) * 128]) for g in range(NG)],
                    [BT.reg(), CT.reg()], [pcb.reg()])
                TT("dve", cbTm.v3(128), pcb.v3(128), bc(triu.ap, 1, [128, NG, 128]), ALU.mult,
                   [pcb.reg(), triu.reg()], [cbTm.reg()])
                fill()
                for q in range(4):
                    pseg = PSF(1, 512)
                    MM(pseg.ap, [(lmat.ap, rhs_seg.ap[:, q * 512:(q + 1) * 512])],
                       [lmat.reg(), rhs_seg.reg(q * 512, (q + 1) * 512)], [pseg.reg()])
                    ACT(Emat.ap[:, q * 512:(q + 1) * 512], pseg.ap, AF.Exp, [pseg.reg()], [Emat.reg(q * 512, (q + 1) * 512)])
                TT("dve", MT.v4(4, 128), Emat.v4(4, 128), bc(cbTm.v3(128), 2, [128, NG, 4, 128]), ALU.mult,
                   [Emat.reg(), cbTm.reg()], [MT.reg()])
                fill()
                py = PSF(3, D)
                MMS([(py.ap[:, h * HP:(h + 1) * HP], MT.ap[:, h * 128:(h + 1) * 128], xdt.ap[:, h * HP:(h + 1) * HP])
                     for h in range(NH)], [MT.reg(), xdt.reg()], [py.reg()])

            def ssd_mid(b):
                blk = t0 + b
                dec = decs[b]
                tb = t1s[b % 2]
                if kinds[blk] in ("A", "B"):
                    state_load(c_state[samp_of[blk]])
                fill()
                po = PSF(5, D)
                py = PSF(3, D)
                MMS([(po.ap[:, g * 256:(g + 1) * 256], CT.ap[:, g * NMAX + b * 128:g * NMAX + (b + 1) * 128],
                      hTb.ap[:, g * 256:(g + 1) * 256]) for g in range(NG)], [CT.reg(), hTb.reg()], [po.reg()])
                ea_ap, ea_reg = dec["ea"]
                TT("dve", tb.ap.rearrange("p (h q) -> p h q", q=HP), po.ap.rearrange("p (h q) -> p h q", q=HP),
                   bc(ea_ap, 2, [128, NH, HP]), ALU.mult, [po.reg(), ea_reg], [tb.reg()])
                TT("dve", tb.ap, py.ap, tb.ap, ALU.add, [py.reg(), tb.reg()], [tb.reg()])
                xt_ap = xtok.ap[:, b * D:(b + 1) * D]
                xt_reg = xtok.reg(b * D, (b + 1) * D)
                w2_ap, w2_reg = dec["w2"]
                TT("pool", xdts.ap.rearrange("p (h q) -> p h q", q=HP), xt_ap.rearrange("p (h q) -> p h q", q=HP),
                   bc(w2_ap, 2, [128, NH, HP]), ALU.mult, [xt_reg, w2_reg], [xdts.reg()])
                pSt = PSF(5, D)
                MMS([(pSt.ap[:, g * 256:(g + 1) * 256], Btok.ap[:, b * NG * 128 + g * 128:b * NG * 128 + (g + 1) * 128],
                      xdts.ap[:, g * 256:(g + 1) * 256]) for g in range(NG)],
                    [Btok.reg(b * NG * 128, (b + 1) * NG * 128), xdts.reg()], [pSt.reg()])
                et_ap, et_reg = dec["eatot"]
                TT("pool", hT.ap.rearrange("p (h q) -> p h q", q=HP), hT.ap.rearrange("p (h q) -> p h q", q=HP),
                   bc(et_ap, 2, [128, NH, HP]), ALU.mult, [hT.reg(), et_reg], [hT.reg()])
                TT("dve", hT.ap, hT.ap, pSt.ap, ALU.add, [hT.reg(), pSt.reg()], [hT.reg()])
                CP("act", hTb.ap, hT.ap, [hT.reg()], [hTb.reg()])
                if kinds[blk] in ("A", "B"):
                    state_export(o_state_s[samp_of[blk]])
                    if kinds[blk + 1] not in ("A", "B"):
                        if NPRE:
                            DMA("sp", hT.ap, hpre_d, reads=[dreg("hpre")], writes=[hT.reg()])
                        else:
                            MEMSET("dve", hT, 0.0)
                        CP("act", hTb.ap, hT.ap, [hT.reg()], [hTb.reg()])
                if blk == last_main:
                    state_export(o_state_p)

            def ssd_back(b):
                tb = t1s[b % 2]
                xt_ap = xtok.ap[:, b * D:(b + 1) * D]
                xt_reg = xtok.reg(b * D, (b + 1) * D)
                TT("pool", t2.ap.rearrange("p (h q) -> p h q", q=HP), xt_ap.rearrange("p (h q) -> p h q", q=HP),
                   bc(Dsk_ap, 2, [128, NH, HP]), ALU.mult, [xt_reg, hvec.reg()], [t2.reg()])
                TT("pool", tb.ap, tb.ap, t2.ap, ALU.add, [tb.reg(), t2.reg()], [tb.reg()])
                TT("dve", tb.ap, tb.ap, sz.ap[:, b * D:(b + 1) * D], ALU.mult, [tb.reg(), sz.reg(b * D, (b + 1) * D)],
                   [tb.reg()])
                ms = small.ap[:, 12:13]
                ACT(ygn.ap, tb.ap, AF.Square, [tb.reg()], [ygn.reg(), small.reg(12, 13)], scale=1.0 / 32.0, accum=ms)
                ACT(small.ap[:, 13:14], ms, AF.Ln, [small.reg(12, 13), epsc.reg()], [small.reg(13, 14)], bias=epsc.ap)
                ACT(small.ap[:, 14:15], small.ap[:, 13:14], AF.Exp, [small.reg(13, 14)], [small.reg(14, 15)], scale=-0.5)
                TS("dve", ygn.ap, tb.ap, small.ap[:, 14:15], ALU.mult, [tb.reg(), small.reg(14, 15)], [ygn.reg()])
                fill()
                pT = PSF(0, KC * 128, BF16)
                TR([(pT.ap[:, k * 128:(k + 1) * 128], ygn.ap[:, k * 128:(k + 1) * 128], identb.ap) for k in range(KC)],
                   [ygn.reg(), identb.reg()], [pT.reg()])
                TT("dve", obv[:, :, b * 128:(b + 1) * 128], pT.v3(128), bc(g_ssm.ap, 2, [128, KC, 128]), ALU.mult,
                   [pT.reg(), g_ssm.reg()], [outb.reg(k * NMAX + b * 128, k * NMAX + (b + 1) * 128) for k in range(KC)])

            ssd_front(0)
            for b in range(nb):
                ssd_mid(b)
                if b + 1 < nb:
                    ssd_front(b + 1)
                ssd_back(b)
            fill(len(fill_q))
            for b in range(nb):
                blk = t0 + b
                if kinds[blk] in ("A", "B"):
                    s_ = samp_of[blk]
                    DMA("sp", cacheTok.ap[0:HA, :], c_conv_a[s_], writes=[cacheTok.reg()])
                    pi = PS(CA * HA, F32)
                    TR([(pi.ap[:, c * HA:(c + 1) * HA], cacheTok.ap[0:HA, c * 128:(c + 1) * 128], identf.ap[0:HA, 0:HA])
                        for c in range(CA)], [cacheTok.reg(), identf.reg()], [pi.reg()])
                    CP("dve", uv[:, :, HA + b * 128:HA + b * 128 + HA], pi.v3(HA), [pi.reg()], [u_ext.reg()])
            CP("pool", uh.v3(HA), uv[:, :, N:N + HA], [u_ext.reg()], [uh.reg()])

            ucfv = ucf.v3(NMAX)
            ucbv = ucb.v3(NMAX)
            sqbv = sqb.v3(NMAX)
            for c in range(CA):
                dg = build_diag("A", c)
                pp = PS(N, F32)
                MM(pp.ap, [(dg.v3(128)[:, k, :], uv[:, c, k:k + N]) for k in range(WA)],
                   [dg.reg(), u_ext.reg(c * UW, (c + 1) * UW)], [pp.reg()])
                rr = ucf.reg(c * NMAX, c * NMAX + N)
                ACT(ucfv[:, c, 0:N], pp.ap, AF.Identity, [pp.reg(), bA.reg()], [rr], bias=bA.ap[:, c:c + 1])
                CP("pool", ucbv[:, c, 0:N], ucfv[:, c, 0:N], [rr], [ucb.reg(c * NMAX, c * NMAX + N)])
                TT("dve", sqbv[:, c, 0:N], ucfv[:, c, 0:N], ucfv[:, c, 0:N], ALU.mult, [rr],
                   [sqb.reg(c * NMAX, c * NMAX + N)])
            p1 = PS(N, F32)
            p2 = PS(N, F32)
            MM(p1.ap, [(onesb.ap, ucbv[:, c, 0:N]) for c in range(CA)], [onesb.reg(), ucb.reg()], [p1.reg()])
            MM(p2.ap, [(onesb.ap, sqbv[:, c, 0:N]) for c in range(CA)], [onesb.reg(), sqb.reg()], [p2.reg()])
            ACT(mu.ap[:, 0:N], p1.ap, AF.Copy, [p1.reg()], [mu.reg(0, N)], scale=1.0 / D)
            TT("dve", rstdN.ap[:, 0:N], mu.ap[:, 0:N], mu.ap[:, 0:N], ALU.mult, [mu.reg(0, N)], [rstdN.reg(0, N)])
            STT("dve", rstdN.ap[:, 0:N], p2.ap, 1.0 / D, rstdN.ap[:, 0:N], ALU.mult, ALU.subtract,
                [p2.reg(), rstdN.reg(0, N)], [rstdN.reg(0, N)])
            ACT(rstdN.ap[:, 0:N], rstdN.ap[:, 0:N], AF.Ln, [rstdN.reg(0, N), epsc.reg()], [rstdN.reg(0, N)], bias=epsc.ap)
            ACT(rstdN.ap[:, 0:N], rstdN.ap[:, 0:N], AF.Exp, [rstdN.reg(0, N)], [rstdN.reg(0, N)], scale=-0.5)
            oav = outa.v3(NMAX)

            def ln_apply(c):
                rr = ucf.reg(c * NMAX, c * NMAX + N)
                TT("dve", ucfv[:, c, 0:N], ucfv[:, c, 0:N], mu.ap[:, 0:N], ALU.subtract, [rr, mu.reg(0, N)], [rr])
                TT("pool", ucfv[:, c, 0:N], ucfv[:, c, 0:N], rstdN.ap[:, 0:N], ALU.mult, [rr, rstdN.reg(0, N)], [rr])
                ACT(oav[:, c, 0:N], ucfv[:, c, 0:N], AF.Silu, [rr, lng.reg(), lnb.reg()],
                    [outa.reg(c * NMAX, c * NMAX + N)], bias=lnb.ap[:, c:c + 1], scale=lng.ap[:, c:c + 1])

            for c in range(CA):
                ln_apply(c)

            obanks = [[PS(512, F32) for _ in range(nb)] for _ in range(2)]
            for half in range(2):
                wo2, ro2 = load_piece(wb_out, "wb_out", D, KC, half * 512, 512)
                for b in range(nb):
                    MM(obanks[half][b].ap, [(obv[:, k, b * 128:(b + 1) * 128], wo2[:, k, :]) for k in range(KC)],
                       [ro2, outb.reg()], [obanks[half][b].reg()], start=True, stop=False)
            hfT = R1
            for half in range(2):
                wo, ro = load_piece(wb_out, "wb_out", 0, KC, half * 512, 512)
                for b in range(nb):
                    MM(obanks[half][b].ap, [(oav[:, k, b * 128:(b + 1) * 128], wo[:, k, :]) for k in range(KC)],
                       [ro] + [outa.reg(k * NMAX + b * 128, k * NMAX + (b + 1) * 128) for k in range(KC)],
                       [obanks[half][b].reg()], start=False, stop=True)
                    xr = xg.reg(b * D + half * 512, b * D + (half + 1) * 512)
                    xa = xg.ap[:, b * D + half * 512:b * D + (half + 1) * 512]
                    TT("dve", xa, xa, obanks[half][b].ap, ALU.add, [xr, obanks[half][b].reg()], [xr])
                    if half == 1:
                        norm_chain_block(b)
                        if b >= 1:
                            norm_tr_block(b - 1, g_ffn, hfT, NMAX)
            norm_tr_block(nb - 1, g_ffn, hfT, NMAX)

            gv = gate_ext.v3(NMAX + HG)
            GW = NMAX + HG
            av = actT.v3(NMAX)
            CP("pool", gv[:, :, 0:HG], gh.v3(HG), [gh.reg()], [gate_ext.reg()])
            npc = (DFF + 511) // 512
            for pi_ in range(npc):
                ncols = min(512, DFF - pi_ * 512)
                wg, rg = load_piece(wb_up, "wb_up", 0, KC, pi_ * 512, ncols)
                for j in range(ncols // 128):
                    c = pi_ * 4 + j
                    pp = PS(N, F32)
                    MM(pp.ap, [(wg[:, k, j * 128:(j + 1) * 128], hv[:, k, 0:N]) for k in range(KC)],
                       [rg, hfT.reg()], [pp.reg()])
                    CP("act" if c % 2 == 0 else "dve", gv[:, c, HG:HG + N], pp.ap, [pp.reg()],
                       [gate_ext.reg(c * GW + HG, c * GW + HG + N)])
                for (b, (r0, nr)) in tails:
                    c0 = b * 128 + r0 + nr - HG
                    pp = PS(512, F32)
                    MM(pp.ap[0:HG, 0:ncols], [(hv[:, k, c0:c0 + HG], wg[:, k, :]) for k in range(KC)],
                       [rg, hfT.reg()], [pp.reg()])
                    CP("dve", cacheTok.ap[0:HG, 0:ncols], pp.ap[0:HG, 0:ncols], [pp.reg()], [cacheTok.reg()])
                    blk = t0 + b
                    dst = o_ffn_s[samp_of[blk]] if kinds[blk] in ("A", "B") else o_ffn_p
                    DMA("pool", dst[:, pi_ * 512:pi_ * 512 + ncols], cacheTok.ap[0:HG, 0:ncols], reads=[cacheTok.reg()])
            for b in range(nb):
                blk = t0 + b
                if kinds[blk] in ("A", "B"):
                    s = samp_of[blk]
                    for (cs, ce) in ((0, 8), (8, 16), (16, 22)):
                        ncx = ce - cs
                        DMA("sp", cacheTok.ap[0:HG, 0:ncx * 128], c_ffn[s, :, cs * 128:ce * 128], writes=[cacheTok.reg()])
                        pi = PS(8 * HG, F32)
                        TR([(pi.ap[:, c * HG:(c + 1) * HG], cacheTok.ap[0:HG, c * 128:(c + 1) * 128],
                             identf.ap[0:HG, 0:HG]) for c in range(ncx)], [cacheTok.reg(), identf.reg()], [pi.reg()])
                        CP("dve", gv[:, cs:ce, HG + b * 128 + HA - HG:HG + b * 128 + HA], pi.v3(HG)[:, 0:ncx, :],
                           [pi.reg()], [gate_ext.reg()])
                if kinds[blk] == "H":
                    sl = gv[:, :, HG + (b + 1) * 128 - HG:HG + (b + 1) * 128]
                    TS("dve", sl, sl, gkeep.ap[:, 0:1], ALU.mult, [gate_ext.reg(), gkeep.reg()], [gate_ext.reg()])
            CP("pool", gh.v3(HG), gv[:, :, N:N + HG], [gate_ext.reg()], [gh.reg()])
            for pi_ in range(npc):
                ncols = min(512, DFF - pi_ * 512)
                wv, rv = load_piece(wb_up, "wb_up", 0, KC, DFF + pi_ * 512, ncols)
                for j in range(ncols // 128):
                    c = pi_ * 4 + j
                    dg = build_diag("F", c)
                    pc_ = PS(N, F32)
                    MM(pc_.ap, [(dg.v3(128)[:, k, :], gv[:, c, k:k + N]) for k in range(WF)],
                       [dg.reg(), gate_ext.reg(c * GW, (c + 1) * GW)], [pc_.reg()])
                    tf = tmpf[c % 2]
                    ACT(tf.ap[:, 0:N], pc_.ap, AF.Silu, [pc_.reg(), bF.reg()], [tf.reg(0, N)], bias=bF.ap[:, c:c + 1])
                    pv_ = PS(N, F32)
                    MM(pv_.ap, [(wv[:, k, j * 128:(j + 1) * 128], hv[:, k, 0:N]) for k in range(KC)],
                       [rv, hfT.reg()], [pv_.reg()])
                    TT("dve", av[:, c, 0:N], pv_.ap, tf.ap[:, 0:N], ALU.mult, [pv_.reg(), tf.reg(0, N)],
                       [actT.reg(c * NMAX, c * NMAX + N)])
            def final_block(b):
                o = 416 + 3 * b
                xa = xg.ap[:, b * D:(b + 1) * D]
                xr = xg.reg(b * D, (b + 1) * D)
                ya = yst.ap[:, b * D:(b + 1) * D]
                yr = yst.reg(b * D, (b + 1) * D)
                ACT(MT.ap[:, 0:D], xa, AF.Square, [xr], [MT.reg(0, D), small.reg(o, o + 1)], scale=1.0 / 32.0,
                    accum=small.ap[:, o:o + 1])
                ACT(small.ap[:, o + 1:o + 2], small.ap[:, o:o + 1], AF.Ln, [small.reg(o, o + 1), epsc.reg()],
                    [small.reg(o + 1, o + 2)], bias=epsc.ap)
                ACT(small.ap[:, o + 2:o + 3], small.ap[:, o + 1:o + 2], AF.Exp, [small.reg(o + 1, o + 2)],
                    [small.reg(o + 2, o + 3)], scale=-0.5)
                STT("dve", ya, xa, small.ap[:, o + 2:o + 3], nfin.ap, ALU.mult, ALU.mult,
                    [xr, small.reg(o + 2, o + 3), nfin.reg()], [yr])
                DMA("pool", y_d[(t0 + b) * 128:(t0 + b + 1) * 128, :], ya, reads=[yr])

            kgs = ((0, 8), (8, 8), (16, 6))
            for half in range(2):
                banks = [PS(512, F32) for _ in range(nb)]
                for gi_, (k0, nk) in enumerate(kgs):
                    wdn, rdn = load_piece(wb_down, "wb_down", k0 * 128, nk, half * 512, 512)
                    for b in range(nb):
                        MM(banks[b].ap, [(av[:, k0 + k, b * 128:(b + 1) * 128], wdn[:, k, :]) for k in range(nk)],
                           [rdn, actT.reg()], [banks[b].reg()], start=(gi_ == 0), stop=(gi_ == len(kgs) - 1))
                        if gi_ == len(kgs) - 1:
                            xr = xg.reg(b * D + half * 512, b * D + (half + 1) * 512)
                            xa = xg.ap[:, b * D + half * 512:b * D + (half + 1) * 512]
                            TT("dve", xa, xa, banks[b].ap, ALU.add, [xr, banks[b].reg()], [xr])
                            if half == 1:
                                final_block(b)

        S.emit()
        build_nc.last_stats = S.stats
    return nc


def _cfg(segtok, npre_blocks):
    nmain = segtok // 128
    kinds = ["A", "B", "H"] + ["M"] * nmain
    groups = [[0, 1, 2]]
    i = 3
    while i < len(kinds):
        n = min(4, len(kinds) - i)
        groups.append(list(range(i, i + n)))
        i += n
    return dict(blocks=kinds, groups=groups, npre=npre_blocks)


def _colvec(v, nchunk):
    v = np.asarray(v, np.float32).reshape(nchunk, 128)
    return np.ascontiguousarray(v.T)


def _convw(w, nchunk):
    W = w.shape[0]
    a = np.asarray(w, np.float32).reshape(W, nchunk, 128)
    return np.ascontiguousarray(a.transpose(2, 1, 0).reshape(128, nchunk * W))


def run_step(inp, seq_len, nseg):
    xp = np.asarray(inp["x_prompt"], np.float32)
    xsamp = np.asarray(inp["x_sample"], np.float32)
    nb_p = xp.shape[0]
    n_s = xsamp.shape[0]
    dec_seq = xsamp.shape[1]
    n_cores = nb_p * nseg
    assert n_s == 2 * n_cores and dec_seq == 16
    segtok = seq_len // nseg
    npre = segtok // 128 + 1
    cfg = _cfg(segtok, npre)
    cfg["ncores"] = n_cores
    cfg["nseg"] = nseg
    nc = build_nc(cfg)
    kinds = cfg["blocks"]
    NBLK = len(kinds)

    common = dict(
        w_in=np.ascontiguousarray(inp["w_in"][0], np.float32),
        w_out=np.ascontiguousarray(inp["w_out"][0], np.float32),
        w_up=np.ascontiguousarray(inp["w_up"][0], np.float32),
        w_down=np.ascontiguousarray(inp["w_down"][0], np.float32),
        norm_mix=_colvec(inp["norm_mix"][0], KC),
        norm_ffn=_colvec(inp["norm_ffn"][0], KC),
        ssm_norm=_colvec(inp["ssm_norm"][0], KC),
        norm_final=np.asarray(inp["norm_final"], np.float32).reshape(1, D),
        wA=_convw(inp["conv_a_w"][0], CA), bA=_colvec(inp["conv_a_b"][0], CA),
        lng=_colvec(inp["ln_a_g"][0], CA), lnb=_colvec(inp["ln_a_b"][0], CA),
        wS=_convw(inp["ssm_conv_w"][0], CX), bS=_colvec(inp["ssm_conv_b"][0], CX),
        wF=_convw(inp["ffn_conv_w"][0], CF), bF=_colvec(inp["ffn_conv_b"][0], CF),
        hvec=np.concatenate([np.asarray(inp["dt_bias"][0], np.float32), np.asarray(inp["a_log"][0], np.float32),
                             np.asarray(inp["d_skip"][0], np.float32)]).reshape(1, 3 * NH),
        ident=np.eye(128, dtype=np.float32),
        triu=np.triu(np.ones((128, 128), np.float32)),
        lmat=np.tril(np.ones((128, 128), np.float32), -1),
    )
    in_maps = []
    for c in range(n_cores):
        seq, k = c // nseg, c % nseg
        start = k * segtok
        xs = np.zeros((NBLK * 128, D), np.float32)
        tmask = np.zeros((128, NBLK), np.float32)
        for i in range(2):
            xs[i * 128 + HA:i * 128 + HA + 16] = xsamp[2 * c + i]
            tmask[HA:HA + 16, i] = 1.0
        if k > 0:
            xs[2 * 128:3 * 128] = xp[seq, start - 128:start]
            tmask[126:128, 2] = 1.0
        xs[3 * 128:] = xp[seq, start:start + segtok]
        tmask[:, 3:] = 1.0
        m = dict(common)
        m.update(xs=xs, tmask=tmask, gkeep=np.full((128, 1), 1.0 if k > 0 else 0.0, np.float32),
                 c_conv_a=np.ascontiguousarray(inp["cache_conv_a"][0, 2 * c:2 * c + 2], np.float32),
                 c_ssm=np.ascontiguousarray(inp["cache_ssm_conv"][0, 2 * c:2 * c + 2], np.float32),
                 c_state=np.ascontiguousarray(inp["state_ssm"][0, 2 * c:2 * c + 2], np.float32).reshape(2, NH * HP, 128),
                 c_ffn=np.ascontiguousarray(inp["cache_ffn_conv"][0, 2 * c:2 * c + 2], np.float32))
        pm = np.zeros((128, npre), np.float32)
        if k > 0:
            pm[126:128, 0] = 1.0
        pm[:, 1:] = 1.0
        pm[126:128, npre - 1] = 0.0
        m["pmask"] = pm
        cm = np.zeros((128, nseg), np.float32)
        cm[:, :k] = 1.0
        m["cmask"] = cm
        in_maps.append(m)
    res = run_bass_kernel_spmd(nc, in_maps, core_ids=list(range(n_cores)))
    R = res.results
    y_prompt = np.zeros((nb_p, seq_len, D), np.float32)
    y_sample = np.zeros((n_s, 16, D), np.float32)
    ca_p = np.zeros((1, nb_p, HA, D), np.float32)
    sc_p = np.zeros((1, nb_p, HX, 2048), np.float32)
    st_p = np.zeros((1, nb_p, NH, HP, 128), np.float32)
    fc_p = np.zeros((1, nb_p, HG, DFF), np.float32)
    ca_s = np.zeros((1, n_s, HA, D), np.float32)
    sc_s = np.zeros((1, n_s, HX, 2048), np.float32)
    st_s = np.zeros((1, n_s, NH, HP, 128), np.float32)
    fc_s = np.zeros((1, n_s, HG, DFF), np.float32)
    for c in range(n_cores):
        seq, k = c // nseg, c % nseg
        start = k * segtok
        r = R[c]
        y = np.asarray(r["y"])
        y_prompt[seq, start:start + segtok] = y[3 * 128:]
        for i in range(2):
            y_sample[2 * c + i] = y[i * 128 + HA:i * 128 + HA + 16]
        ca_s[0, 2 * c:2 * c + 2] = np.asarray(r["o_conv_a_s"])
        sc_s[0, 2 * c:2 * c + 2] = np.asarray(r["o_ssm_s"])
        st_s[0, 2 * c:2 * c + 2] = np.asarray(r["o_state_s"]).reshape(2, NH, HP, 128)
        fc_s[0, 2 * c:2 * c + 2] = np.asarray(r["o_ffn_s"])
        if k == nseg - 1:
            ca_p[0, seq] = np.asarray(r["o_conv_a_p"])
            sc_p[0, seq] = np.asarray(r["o_ssm_p"])
            st_p[0, seq] = np.asarray(r["o_state_p"]).reshape(NH, HP, 128)
            fc_p[0, seq] = np.asarray(r["o_ffn_p"])
    return (y_prompt, y_sample, ca_p, sc_p, st_p, fc_p, ca_s, sc_s, st_s, fc_s)


def kernel(**inputs):
    inp = {k: np.asarray(v) for k, v in inputs.items()}
    return run_step(inp, SEQ, NSEG)
```
